# Optimizing a Trainium2 kernel written in Bass

```python
import math
import jax, jax.numpy as jnp
from jax import lax
import numpy as np

D_MODEL = 4096
BATCH = 4
SEQ = 2048
DEPTH = 1

HEAD_DIM = 128
MIX_WIDTH = D_MODEL
SWA_HEADS = (MIX_WIDTH // 2) // HEAD_DIM
SWA_KV_HEADS = SWA_HEADS // 4
SWA_GROUP = SWA_HEADS // SWA_KV_HEADS
WINDOW = 128
BLOCK = 128
DIFF_HEADS = (MIX_WIDTH // 2) // HEAD_DIM
DIFF_QK_DIM = HEAD_DIM // 2
DIFF_V_DIM = HEAD_DIM
D_FF = 4 * D_MODEL
PLE_DIM = 256
ROPE_THETA = 10000.0
NORM_EPS = 1e-6
NEG_INF = -1e30

SWA_Q_COLS = SWA_HEADS * HEAD_DIM
SWA_KV_COLS = SWA_KV_HEADS * HEAD_DIM
DIFF_QK_COLS = 2 * DIFF_HEADS * DIFF_QK_DIM
DIFF_V_COLS = DIFF_HEADS * DIFF_V_DIM
IN_COLS = SWA_Q_COLS + 2 * SWA_KV_COLS + 2 * DIFF_QK_COLS + DIFF_V_COLS

kernel_name = "hymba_swa_sink_diffattn_sqrelu_ple"


def rmsnorm(x, g):
    xf = x.astype(jnp.float32)
    y = xf * lax.rsqrt(jnp.mean(xf * xf, axis=-1, keepdims=True) + NORM_EPS)
    return (y * g.astype(jnp.float32)).astype(x.dtype)


def rope_tables(seq_len, dim):
    pos = jnp.arange(seq_len, dtype=jnp.float32)
    inv = 1.0 / (ROPE_THETA ** (jnp.arange(0, dim, 2, dtype=jnp.float32) / dim))
    ang = pos[:, None] * inv[None, :]
    ang = jnp.concatenate([ang, ang], axis=-1)
    return jnp.cos(ang), jnp.sin(ang)


def apply_rope(x, cos, sin):
    shape = (1, cos.shape[0]) + (1,) * (x.ndim - 3) + (cos.shape[1],)
    c, s = cos.reshape(shape), sin.reshape(shape)
    xf = x.astype(jnp.float32)
    x1, x2 = jnp.split(xf, 2, axis=-1)
    rot = jnp.concatenate([-x2, x1], axis=-1)
    return (xf * c + rot * s).astype(x.dtype)


def sliding_window_sink_attention(q, k, v, sinks):
    B, S = q.shape[0], q.shape[1]
    nb = S // BLOCK
    qb = q.reshape(B, nb, BLOCK, SWA_KV_HEADS, SWA_GROUP, HEAD_DIM)
    kb = k.reshape(B, nb, BLOCK, SWA_KV_HEADS, HEAD_DIM)
    vb = v.reshape(B, nb, BLOCK, SWA_KV_HEADS, HEAD_DIM)
    pad = lambda t: jnp.concatenate([jnp.zeros_like(t[:, :1]), t[:, :-1]], axis=1)
    kk = jnp.concatenate([pad(kb), kb], axis=2)
    vv = jnp.concatenate([pad(vb), vb], axis=2)
    scale = 1.0 / math.sqrt(HEAD_DIM)
    s = jnp.einsum('bnqkgd,bnjkd->bnkgqj', qb, kk).astype(jnp.float32) * scale
    blk = jnp.arange(nb)[:, None, None] * BLOCK
    qpos = blk + jnp.arange(BLOCK)[None, :, None]
    kpos = blk - BLOCK + jnp.arange(2 * BLOCK)[None, None, :]
    valid = (kpos <= qpos) & (qpos - kpos < WINDOW) & (kpos >= 0)
    s = jnp.where(valid[None, :, None, None], s, NEG_INF)
    sink = jnp.broadcast_to(
        sinks.astype(jnp.float32).reshape(1, 1, SWA_KV_HEADS, SWA_GROUP, 1, 1),
        s.shape[:-1] + (1,))
    probs = jax.nn.softmax(jnp.concatenate([s, sink], axis=-1), axis=-1)[..., :-1]
    o = jnp.einsum('bnkgqj,bnjkd->bnqkgd', probs.astype(v.dtype), vv)
    return o.reshape(B, S, SWA_HEADS * HEAD_DIM)


def differential_attention(q, k, v, lam):
    B, S = q.shape[0], q.shape[1]
    nb = S // BLOCK
    qb = q.reshape(B, nb, BLOCK, 2, DIFF_HEADS, DIFF_QK_DIM).transpose(1, 0, 2, 3, 4, 5)
    kpos = jnp.arange(S)
    scale = 1.0 / math.sqrt(DIFF_QK_DIM)

    def one_block(args):
        qi, i = args
        s = jnp.einsum('bqchd,bkchd->bchqk', qi, k).astype(jnp.float32) * scale
        qpos = i * BLOCK + jnp.arange(BLOCK)
        causal = kpos[None, :] <= qpos[:, None]
        s = jnp.where(causal, s, NEG_INF)
        pr = jax.nn.softmax(s, axis=-1)
        w = (pr[:, 0] - lam * pr[:, 1]).astype(v.dtype)
        return jnp.einsum('bhqk,bkhd->bqhd', w, v)

    o = lax.map(one_block, (qb, jnp.arange(nb)))
    return o.transpose(1, 0, 2, 3, 4).reshape(B, S, DIFF_HEADS, DIFF_V_DIM)


def setup_inputs(seed: int = 0) -> dict:
    key = jax.random.key(seed)
    ks = jax.random.split(key, 24)
    nrm = lambda k, shape, sc: jax.random.normal(k, shape, jnp.float32) * sc
    gain = lambda k, n: 1.0 + 0.02 * jax.random.normal(k, (DEPTH, n), jnp.float32)
    return {
        "x": nrm(ks[0], (BATCH, SEQ, D_MODEL), 1.0),
        "p": nrm(ks[1], (DEPTH, BATCH, SEQ, PLE_DIM), 1.0),
        "attn_norm_g": gain(ks[2], D_MODEL),
        "w_in": nrm(ks[3], (DEPTH, D_MODEL, IN_COLS), D_MODEL ** -0.5),
        "swa_q_norm_g": gain(ks[4], HEAD_DIM),
        "swa_k_norm_g": gain(ks[5], HEAD_DIM),
        "swa_sinks": nrm(ks[6], (DEPTH, SWA_HEADS), 0.5),
        "diff_q_norm_g": gain(ks[7], DIFF_QK_DIM),
        "diff_k_norm_g": gain(ks[8], DIFF_QK_DIM),
        "diff_lambda_q1": nrm(ks[9], (DEPTH, DIFF_QK_DIM), 0.1),
        "diff_lambda_k1": nrm(ks[10], (DEPTH, DIFF_QK_DIM), 0.1),
        "diff_lambda_q2": nrm(ks[11], (DEPTH, DIFF_QK_DIM), 0.1),
        "diff_lambda_k2": nrm(ks[12], (DEPTH, DIFF_QK_DIM), 0.1),
        "diff_subln_g": gain(ks[13], DIFF_V_DIM),
        "w_o": nrm(ks[14], (DEPTH, MIX_WIDTH, D_MODEL), MIX_WIDTH ** -0.5),
        "mlp_norm_g": gain(ks[15], D_MODEL),
        "w_up": nrm(ks[16], (DEPTH, D_MODEL, D_FF), D_MODEL ** -0.5),
        "w_down": nrm(ks[17], (DEPTH, D_FF, D_MODEL), D_FF ** -0.5),
        "w_ple_proj": nrm(ks[18], (DEPTH, PLE_DIM, D_MODEL), PLE_DIM ** -0.5),
        "ple_norm_g": gain(ks[19], D_MODEL),
        "w_ple_gate": nrm(ks[20], (DEPTH, D_MODEL, D_MODEL), D_MODEL ** -0.5),
    }


def reference(x, p, attn_norm_g, w_in, swa_q_norm_g, swa_k_norm_g, swa_sinks,
              diff_q_norm_g, diff_k_norm_g, diff_lambda_q1, diff_lambda_k1,
              diff_lambda_q2, diff_lambda_k2, diff_subln_g, w_o, mlp_norm_g,
              w_up, w_down, w_ple_proj, ple_norm_g, w_ple_gate):
    B, S, _ = x.shape
    cos_a, sin_a = rope_tables(S, HEAD_DIM)
    cos_b, sin_b = rope_tables(S, DIFF_QK_DIM)
    splits = np.cumsum([SWA_Q_COLS, SWA_KV_COLS, SWA_KV_COLS,
                        DIFF_QK_COLS, DIFF_QK_COLS])
    h = x
    for i in range(DEPTH):
        a = rmsnorm(h, attn_norm_g[i])
        proj = a @ w_in[i]
        qa, ka, va, qb, kb, vb = jnp.split(proj, splits, axis=-1)

        qa = qa.reshape(B, S, SWA_HEADS, HEAD_DIM)
        ka = ka.reshape(B, S, SWA_KV_HEADS, HEAD_DIM)
        va = va.reshape(B, S, SWA_KV_HEADS, HEAD_DIM)
        qa = apply_rope(rmsnorm(qa, swa_q_norm_g[i]), cos_a, sin_a)
        ka = apply_rope(rmsnorm(ka, swa_k_norm_g[i]), cos_a, sin_a)
        out_a = sliding_window_sink_attention(qa, ka, va, swa_sinks[i])

        qb = qb.reshape(B, S, 2, DIFF_HEADS, DIFF_QK_DIM)
        kb = kb.reshape(B, S, 2, DIFF_HEADS, DIFF_QK_DIM)
        vb = vb.reshape(B, S, DIFF_HEADS, DIFF_V_DIM)
        qb = apply_rope(rmsnorm(qb, diff_q_norm_g[i]), cos_b, sin_b)
        kb = apply_rope(rmsnorm(kb, diff_k_norm_g[i]), cos_b, sin_b)
        lambda_init = 0.8 - 0.6 * math.exp(-0.3 * i)
        lam = (jnp.exp(jnp.sum(diff_lambda_q1[i].astype(jnp.float32) * diff_lambda_k1[i].astype(jnp.float32)))
               - jnp.exp(jnp.sum(diff_lambda_q2[i].astype(jnp.float32) * diff_lambda_k2[i].astype(jnp.float32)))
               + lambda_init)
        ob = differential_attention(qb, kb, vb, lam)
        out_b = (rmsnorm(ob, diff_subln_g[i]) * (1.0 - lambda_init)).reshape(B, S, DIFF_V_COLS)

        h = h + jnp.concatenate([out_a, out_b], axis=-1) @ w_o[i]

        m = rmsnorm(h, mlp_norm_g[i])
        h = h + jnp.square(jax.nn.relu(m @ w_up[i])) @ w_down[i]

        pe = rmsnorm(p[i] @ w_ple_proj[i], ple_norm_g[i])
        h = h + jax.nn.sigmoid(h @ w_ple_gate[i]) * pe
    return h
```

```python
import math
import contextlib
import numpy as np
import ml_dtypes
import concourse.bass as bass
import concourse.mybir as mybir
from concourse.bass_utils import run_bass_kernel_spmd

F32 = mybir.dt.float32
BF16 = mybir.dt.bfloat16
AF = mybir.ActivationFunctionType
ALU = mybir.AluOpType
AX = mybir.AxisListType
BF = ml_dtypes.bfloat16

D = 4096
TOK = 1024
FF = 16384
PLE = 256
EPS = 1e-6
NEG = -30000.0
IN_COLS = 9216
C_QA, C_KA, C_VA, C_QB, C_KB, C_VB = 0, 2048, 2560, 3072, 5120, 7168
SCALE_A = 1.0 / math.sqrt(128.0)
SCALE_B = 1.0 / math.sqrt(64.0)
LAMBDA_INIT = 0.8 - 0.6 * math.exp(-0.3 * 0)

P_GATT, P_GMLP, P_GQA, P_GKA, P_GQB, P_GKB, P_GSUB, P_SINK, P_LQ1, P_LK1, P_LQ2, P_LK2, P_VALID = (
    0, 32, 64, 192, 320, 384, 448, 449, 465, 529, 593, 657, 721)
NPAR = 722


class Sem:
    def __init__(self, h):
        self.h = h
        self.n = 0


class Eng:
    def __init__(self, b, e, name):
        self.b = b
        self.e = e
        self.name = name
        self.sem = b.new_sem("e_" + name)
        self.waited = {}

    def wait_ev(self, s, v):
        if v <= 0 or self.waited.get(s, 0) >= v:
            return
        self.e.wait_ge(s.h, v)
        self.waited[s] = v

    def waitd(self, d):
        for s, v in d.items():
            self.wait_ev(s, v)

    def done(self, ins):
        self.sem.n += 1
        ins.then_inc(self.sem.h, 1)
        return (self.sem, self.sem.n)


class Buf:
    __slots__ = ("ready", "readers")

    def __init__(self):
        self.ready = {}
        self.readers = {}


def _merge(d, ev):
    s, v = ev
    if d.get(s, 0) < v:
        d[s] = v


class Builder:
    def __init__(self, nc, es):
        self.nc = nc
        self.es = es
        self.sems = []
        self.pe = Eng(self, nc.tensor, "pe")
        self.act = Eng(self, nc.scalar, "act")
        self.dve = Eng(self, nc.vector, "dve")
        self.sp = Eng(self, nc.sync, "sp")
        self.pool = Eng(self, nc.gpsimd, "pool")
        self.dsems = [self.new_sem("d%d" % i) for i in range(20)]
        self.di = 0
        self.uid = 0

    def new_sem(self, name):
        s = Sem(self.es.enter_context(self.nc.semaphore(name)))
        self.sems.append(s)
        return s

    def sb(self, stack, shape, dt, name=None):
        self.uid += 1
        return stack.enter_context(self.nc.sbuf_tensor("%s_%d" % (name or "t", self.uid), list(shape), dt))

    def op(self, eng, emit, reads=(), writes=(), accum=()):
        for b in reads:
            eng.waitd(b.ready)
        for b in writes:
            eng.waitd(b.readers)
            eng.waitd(b.ready)
        for b in accum:
            eng.waitd(b.readers)
            eng.waitd(b.ready)
        ins = emit()
        ev = eng.done(ins)
        for b in reads:
            _merge(b.readers, ev)
        for b in writes:
            b.ready = {ev[0]: ev[1]}
            b.readers = {}
        for b in accum:
            _merge(b.ready, ev)
        return ev

    def dma(self, out, in_, reads=(), writes=(), accum=()):
        sp = self.sp
        s = self.dsems[self.di % len(self.dsems)]
        self.di += 1
        sp.wait_ev(s, s.n)
        for b in reads:
            sp.waitd(b.ready)
        for b in writes:
            sp.waitd(b.readers)
            sp.waitd(b.ready)
        for b in accum:
            sp.waitd(b.readers)
            sp.waitd(b.ready)
        ins = self.nc.sync.dma_start(out=out, in_=in_)
        s.n += 16
        ins.then_inc(s.h, 16)
        ev = (s, s.n)
        for b in reads:
            _merge(b.readers, ev)
        for b in writes:
            b.ready = {s: s.n}
            b.readers = {}
        for b in accum:
            _merge(b.ready, ev)
        return ev

    def barrier(self):
        for e in (self.pe, self.act, self.dve, self.sp):
            for s in self.sems:
                if s in self.wsems:
                    continue
                e.wait_ev(s, s.n)


class WStream:
    def __init__(self, b, slots_main, slot_extra):
        self.b = b
        self.slot_aps = slots_main + [slot_extra]
        self.slot_buf = [Buf() for _ in self.slot_aps]
        self.sems = [b.new_sem("w%d" % i) for i in range(len(self.slot_aps))]
        b.wsems = set(self.sems)
        self.pieces = []
        self.issued = 0
        self.released = []
        self.cursor = 0
        self.hold = 2
        self.gates = {}

    def plan(self, src, nk, ncols, slot):
        self.pieces.append((src, nk, ncols, slot))
        self.released.append(False)

    def _prev_released(self, j):
        slot = self.pieces[j][3]
        for q in range(j - 1, -1, -1):
            if self.pieces[q][3] == slot:
                return self.released[q]
        return True

    def view(self, j):
        src, nk, ncols, slot = self.pieces[j]
        return self.slot_aps[slot][:, 0:nk * ncols].rearrange("p (k c) -> p k c", k=nk)

    def pump(self):
        while (self.issued < len(self.pieces) and (self.hold is None or self.issued < self.hold)
               and self._prev_released(self.issued)):
            j = self.issued
            src, nk, ncols, slot = self.pieces[j]
            buf = self.slot_buf[slot]
            pool = self.b.pool
            pool.waitd(buf.readers)
            for ev in self.gates.get(j, ()):
                pool.wait_ev(*ev)
            ins = self.b.nc.gpsimd.dma_start(out=self.view(j), in_=src)
            s = self.sems[slot]
            s.n += 16
            ins.then_inc(s.h, 16)
            buf.ready = {s: s.n}
            buf.readers = {}
            self.issued += 1

    def get(self):
        j = self.cursor
        self.cursor += 1
        self.pump()
        assert self.issued > j, "weight piece %d not issued (slot not released)" % j
        return j, self.view(j), self.slot_buf[self.pieces[j][3]]

    def release(self, j):
        self.released[j] = True
        self.pump()


def wsrc(w, r0, nk, c0, ncols):
    return w[r0:r0 + nk * 128, c0:c0 + ncols].rearrange("(k p) c -> p k c", p=128)


def build_program(debug=False):
    nc = bass.Bass("TRN2", target_bir_lowering=False)
    dram_in = lambda name, shape, dt=F32: nc.dram_tensor(name, list(shape), dt, kind="ExternalInput").ap()
    x_own = dram_in("x_own", [TOK, D])
    x_ctx = dram_in("x_ctx", [TOK, D])
    p_own = dram_in("p_own", [TOK, PLE])
    w_in = dram_in("w_in", [D, IN_COLS])
    w_o = dram_in("w_o", [D, D])
    w_up = dram_in("w_up", [D, FF])
    w_down = dram_in("w_down", [FF, D])
    w_ple = dram_in("w_ple", [PLE, D])
    w_gate = dram_in("w_gate", [D, D])
    params = dram_in("params", [128, NPAR])
    gple = dram_in("gple", [128, D])
    NTAB = 2 * 64 + 2 * 8 * 32 + 2 * 8 * 64 + 2 * 8 * 32
    rope = dram_in("rope", [128, NTAB])
    mats = dram_in("mats", [128, 4, 128], BF16)
    masks = dram_in("masks", [128, 7, 512], BF16)
    out = nc.dram_tensor("out", [TOK, D], F32, kind="ExternalOutput").ap()
    skind = dict(kind="ExternalOutput") if debug else {}
    QaT = nc.dram_tensor("QaT", [16, 128, 1024], BF16, **skind).ap()
    KaT = nc.dram_tensor("KaT", [4, 128, 1152], BF16, **skind).ap()
    Va = nc.dram_tensor("Va", [1152, 512], BF16, **skind).ap()
    QbT = nc.dram_tensor("QbT", [16, 128, 1024], BF16, **skind).ap()
    KbT = nc.dram_tensor("KbT", [16, 128, 2048], BF16, **skind).ap()
    Vb = nc.dram_tensor("Vb", [2048, 2048], BF16, **skind).ap()
    catT = nc.dram_tensor("catT", [32, 128, 1024], BF16, **skind).ap()
    wdown_bf = nc.dram_tensor("wdown_bf", [FF, D], BF16).ap()

    with contextlib.ExitStack() as es:
        B = Builder(nc, es)
        pe, act, dve, sp = B.pe, B.act, B.dve, B.sp
        T, V, S = nc.tensor, nc.vector, nc.scalar
        psum = es.enter_context(nc.psum_tensor("psum", [128, 4096], F32))
        bank = [psum[:, i * 512:(i + 1) * 512] for i in range(8)]
        bankb = [Buf() for _ in range(8)]

        wslots = [B.sb(es, [128, 8192], BF16, "wslot") for _ in range(3)]
        par = B.sb(es, [128, NPAR], F32, "par")
        mat = B.sb(es, [128, 4, 128], BF16, "mat")
        stat = B.sb(es, [128, 512], F32, "stat")
        neglam = B.sb(es, [128, 1], F32, "neglam")
        ident, ones1, onesv, onesn = (mat[:, i, :] for i in range(4))
        cbuf = Buf()
        statc = [32, 0]

        def newstat(n=1, persistent=False):
            if persistent:
                c = statc[1]
                statc[1] += n
                assert statc[1] <= 32
                return stat[:, c:c + n]
            if statc[0] + n > 512:
                statc[0] = 32
            c = statc[0]
            statc[0] += n
            return stat[:, c:c + n]

        B.dma(par[:, :], params[:, :], writes=[cbuf])
        B.dma(mat[:, :, :], mats[:, :, :], accum=[cbuf])

        p1 = contextlib.ExitStack()
        wextra = B.sb(p1, [128, 8192], BF16, "wextra")
        ws = WStream(B, wslots, wextra)
        blocks_ctx = [("kb", C_KB + 512 * i) for i in range(4)] + [("vb", C_VB + 512 * i) for i in range(4)] + \
                     [("ka", C_KA), ("va", C_VA)]
        blocks_own = [("qa", C_QA + 512 * i) for i in range(4)] + [("ka", C_KA), ("va", C_VA)] + \
                     [("qb", C_QB + 512 * i) for i in range(4)] + [("kb", C_KB + 512 * i) for i in range(4)] + \
                     [("vb", C_VB + 512 * i) for i in range(4)]
        n = 0
        for (kind, c0) in blocks_ctx + blocks_own:
            for kh in range(2):
                ws.plan(wsrc(w_in, kh * 2048, 16, c0, 512), 16, 512, n % 4)
                n += 1
        n = 0
        down_piece_idx = []
        for t in range(2):
            for cb in range(8):
                ws.plan(wsrc(w_ple, 0, 2, cb * 512, 512), 2, 512, n % 3); n += 1
            for cb in range(8):
                for kh in range(2):
                    ws.plan(wsrc(w_o, kh * 2048, 16, cb * 512, 512), 16, 512, n % 3); n += 1
            def up_pieces(f):
                nonlocal n
                for cbl in range(2):
                    for kh in range(2):
                        ws.plan(wsrc(w_up, kh * 2048, 16, f * 1024 + cbl * 512, 512), 16, 512, n % 3); n += 1
            def down_pieces(f):
                nonlocal n
                for cq in range(4):
                    down_piece_idx.append((len(ws.pieces), f))
                    ws.plan(wsrc(wdown_bf, f * 1024, 8, cq * 1024, 1024), 8, 1024, n % 3); n += 1
            up_pieces(0)
            for f in range(16):
                if f + 1 < 16:
                    up_pieces(f + 1)
                down_pieces(f)
            for cb in range(8):
                ws.plan(wsrc(w_ple, 0, 2, cb * 512, 512), 2, 512, n % 3); n += 1
                for kh in range(2):
                    ws.plan(wsrc(w_gate, kh * 2048, 16, cb * 512, 512), 16, 512, n % 3); n += 1

        conv_sems = [B.new_sem("cv%d" % i) for i in range(4)]
        for s_ in conv_sems:
            B.wsems.add(s_)
        conv_ev = []
        conv_next = [0]

        def conv_step(k=1):
            for _ in range(k):
                i = conv_next[0]
                if i >= 32:
                    return
                conv_next[0] += 1
                cs = conv_sems[i % 4]
                B.pool.wait_ev(cs, cs.n)
                ins = nc.gpsimd.dma_start(out=wdown_bf[i * 512:(i + 1) * 512, :], in_=w_down[i * 512:(i + 1) * 512, :])
                cs.n += 16
                ins.then_inc(cs.h, 16)
                conv_ev.append((cs, cs.n))

        def conv_finish():
            conv_step(32)
            for (pi_, f) in down_piece_idx:
                ws.gates.setdefault(pi_, []).extend([conv_ev[2 * f], conv_ev[2 * f + 1]])

        def mm_group(emit, reads, banks_):
            return B.op(pe, emit, reads=reads, writes=[bankb[i] for i in banks_])

        def build_xT(stack_bufs, get_half, ntc, gcol0, dst, dstbuf, tok0, norm, tbanks, lazy=False):
            xbf, xbfb, junk = stack_bufs

            def chunk(tc):
                dstb = dstbuf[tc] if isinstance(dstbuf, list) else dstbuf
                halves = [get_half(tc, hf) for hf in range(2)]
                if norm:
                    ss2 = newstat(2)
                    ssb = Buf()
                    for hf in range(2):
                        ap_, hb = halves[hf]
                        B.op(act, lambda ap_=ap_, hf=hf: S.activation(
                            out=junk[:, :], in_=ap_, func=AF.Square, scale=1.0 / 64.0,
                            accum_out=ss2[:, hf:hf + 1]), reads=[hb], accum=[ssb])
                    st3 = newstat(3)
                    sb3 = Buf()
                    B.op(dve, lambda: V.tensor_tensor(out=st3[:, 0:1], in0=ss2[:, 0:1], in1=ss2[:, 1:2], op=ALU.add),
                         reads=[ssb], writes=[sb3])
                    B.op(act, lambda: S.activation(out=st3[:, 1:2], in_=st3[:, 0:1], func=AF.Sqrt, bias=EPS, scale=1.0),
                         reads=[sb3], accum=[sb3])
                    B.op(dve, lambda: V.reciprocal(out=st3[:, 2:3], in_=st3[:, 1:2]), reads=[sb3], accum=[sb3])
                    for hf in range(2):
                        ap_, hb = halves[hf]
                        B.op(dve, lambda ap_=ap_, hf=hf: V.tensor_scalar(
                            out=xbf[:, hf * 2048:(hf + 1) * 2048], in0=ap_, scalar1=st3[:, 2:3], scalar2=None,
                            op0=ALU.mult), reads=[hb, sb3], accum=[xbfb])
                else:
                    for hf in range(2):
                        ap_, hb = halves[hf]
                        B.op(act, lambda ap_=ap_, hf=hf: S.copy(out=xbf[:, hf * 2048:(hf + 1) * 2048], in_=ap_),
                             reads=[hb], accum=[xbfb])
                bs = tbanks[tc % 2]
                for q in range(4):
                    bi = bs[q]
                    psb = bank[bi].bitcast(BF16)

                    def emit(q=q, psb=psb):
                        for j in range(8):
                            k = q * 8 + j
                            ins = T.transpose(out=psb[:, j * 128:(j + 1) * 128], in_=xbf[:, k * 128:(k + 1) * 128],
                                              identity=ident)
                        return ins
                    mm_group(emit, [xbfb, cbuf], [bi])
                    o_ap = dst[:, q * 8:(q + 1) * 8, tok0 + tc * 128: tok0 + (tc + 1) * 128]
                    i_ap = psb.rearrange("p (k t) -> p k t", k=8)
                    if gcol0 is not None:
                        gb = par[:, gcol0 + q * 8: gcol0 + (q + 1) * 8].unsqueeze(2).to_broadcast([128, 8, 128])
                        B.op(dve, lambda o_ap=o_ap, i_ap=i_ap, gb=gb: V.tensor_tensor(out=o_ap, in0=i_ap, in1=gb, op=ALU.mult),
                             reads=[bankb[bi], cbuf], accum=[dstb])
                    else:
                        B.op(act, lambda o_ap=o_ap, i_ap=i_ap: S.copy(out=o_ap, in_=i_ap), reads=[bankb[bi]], accum=[dstb])

            state = [0]

            def ensure(tc_upto):
                while state[0] <= min(tc_upto, ntc - 1):
                    chunk(state[0])
                    state[0] += 1
            if lazy:
                return ensure
            ensure(ntc - 1)

        lamt = B.sb(p1, [128, 4, 64], F32, "lamt")
        lamb = Buf()
        B.op(dve, lambda: V.tensor_tensor(out=lamt[:, 0, :], in0=par[:, P_LQ1:P_LQ1 + 64], in1=par[:, P_LK1:P_LK1 + 64], op=ALU.mult),
             reads=[cbuf], accum=[lamb])
        B.op(dve, lambda: V.tensor_tensor(out=lamt[:, 1, :], in0=par[:, P_LQ2:P_LQ2 + 64], in1=par[:, P_LK2:P_LK2 + 64], op=ALU.mult),
             reads=[cbuf], accum=[lamb])
        ls = newstat(6, True)
        lsb = Buf()
        B.op(dve, lambda: V.reduce_sum(out=ls[:, 0:2], in_=lamt[:, 0:2, :], axis=AX.X), reads=[lamb], writes=[lsb])
        B.op(act, lambda: S.activation(out=ls[:, 2:4], in_=ls[:, 0:2], func=AF.Exp), reads=[lsb], accum=[lsb])
        B.op(dve, lambda: V.tensor_tensor(out=ls[:, 4:5], in0=ls[:, 3:4], in1=ls[:, 2:3], op=ALU.subtract), reads=[lsb], accum=[lsb])
        nlb = Buf()
        B.op(dve, lambda: V.tensor_scalar(out=neglam[:, :], in0=ls[:, 4:5], scalar1=-LAMBDA_INIT, scalar2=None, op0=ALU.add),
             reads=[lsb], writes=[nlb])
        esink = newstat(16, True)
        esb = Buf()
        B.op(act, lambda: S.activation(out=esink, in_=par[:, P_SINK:P_SINK + 16], func=AF.Exp), reads=[cbuf], writes=[esb])

        aT = B.sb(p1, [128, 32, 1024], BF16, "aT")
        aTb = [Buf() for _ in range(8)]
        NXST = 3
        xst = [B.sb(p1, [128, 2048], F32, "xst") for _ in range(NXST)]
        xstb = [Buf() for _ in range(NXST)]
        xbf = B.sb(p1, [128, 4096], BF16, "xbf")
        xbfb = Buf()
        junk = B.sb(p1, [128, 2048], BF16, "junk")
        tabs = B.sb(p1, [128, NTAB], F32, "tabs")
        o_ = 0
        tabAc = tabs[:, o_:o_ + 128].rearrange("p (s c d) -> p s c d", s=2, c=1); o_ += 128
        tabBc = tabs[:, o_:o_ + 512].rearrange("p (s c d) -> p s c d", s=2, c=8); o_ += 512
        tabA = tabs[:, o_:o_ + 1024].rearrange("p (s c d) -> p s c d", s=2, c=8); o_ += 1024
        tabB = tabs[:, o_:o_ + 512].rearrange("p (s c d) -> p s c d", s=2, c=8); o_ += 512
        tabb = Buf()
        tab_loaded = [False]
        NSET = 2
        qf = [B.sb(p1, [128, 512], F32, "qf") for _ in range(NSET)]
        sq = [B.sb(p1, [128, 512], F32, "sq") for _ in range(NSET)]
        t2 = [B.sb(p1, [128, 512], F32, "t2") for _ in range(NSET)]
        ob = [B.sb(p1, [128, 512], BF16, "ob") for _ in range(NSET)]
        qfb = [Buf() for _ in range(NSET)]; sqb = [Buf() for _ in range(NSET)]
        t2b = [Buf() for _ in range(NSET)]; obb = [Buf() for _ in range(NSET)]
        tst = [B.sb(p1, [128, 4, 512], BF16, "tst") for _ in range(2)]
        tstb = [Buf() for _ in range(2)]
        vst = [B.sb(p1, [128, 512], BF16, "vst") for _ in range(2)]
        vstb = [Buf() for _ in range(2)]
        scr = dict(QaT=Buf(), KaT=Buf(), Va=Buf(), QbT=Buf(), KbT=Buf(), Vb=Buf(), catT=Buf())

        hcount = [0]
        xload_evs = []

        def make_get_half(xsrc):
            cache = {}

            def issue(tc, hf):
                i = hcount[0] % NXST
                hcount[0] += 1
                xload_evs.append(B.dma(xst[i][:, :], xsrc[tc * 128:(tc + 1) * 128, hf * 2048:(hf + 1) * 2048], writes=[xstb[i]]))
                return xst[i][:, :], xstb[i]

            def get_half(tc, hf):
                if (tc, hf) in cache:
                    return cache.pop((tc, hf))
                return issue(tc, hf)

            def prefetch(tc, hf):
                cache[(tc, hf)] = issue(tc, hf)
            get_half.prefetch = prefetch
            return get_half

        get_halves = [make_get_half(x_ctx), make_get_half(x_own)]

        setc = [0]
        tgc = [0]
        vc = [0]
        grp = [0]

        for pas, (xsrc, blocks) in enumerate([(x_ctx, blocks_ctx), (x_own, blocks_own)]):
            own = pas == 1
            ensure_aT = build_xT((xbf, xbfb, junk), get_halves[pas], 8, P_GATT, aT, aTb, 0, True,
                                 [(0, 1, 2, 3), (0, 1, 2, 3)], lazy=True)

            deferred = []

            def qk_epilogue(kind, c0, tg, tcs, tcx, bi, ti, tbanks, own):
                hd = 128 if kind in ("qa", "ka") else 64
                nh = 512 // hd
                hh = hd // 2
                si = setc[0] % NSET
                setc[0] += 1
                gcol = {"qa": P_GQA, "ka": P_GKA, "qb": P_GQB, "kb": P_GKB}[kind]
                B.op(act, lambda: S.copy(out=qf[si][:, :], in_=bank[bi]), reads=[bankb[bi]], writes=[qfb[si]])
                B.op(act, lambda: S.activation(out=sq[si][:, :], in_=bank[bi], func=AF.Square),
                     reads=[bankb[bi]], writes=[sqb[si]])
                st_ = newstat(3 * nh)
                stb = Buf()
                B.op(dve, lambda: V.reduce_sum(out=st_[:, 0:nh], in_=sq[si][:, :].rearrange("p (h d) -> p h d", h=nh),
                                               axis=AX.X), reads=[sqb[si]], writes=[stb])
                B.op(act, lambda: S.activation(out=st_[:, nh:2 * nh], in_=st_[:, 0:nh], func=AF.Sqrt, bias=EPS,
                                               scale=1.0 / hd), reads=[stb], accum=[stb])
                B.op(dve, lambda: V.reciprocal(out=st_[:, 2 * nh:3 * nh], in_=st_[:, nh:2 * nh]), reads=[stb], accum=[stb])
                q3 = qf[si][:, :].rearrange("p (h d) -> p h d", h=nh)
                s3 = sq[si][:, :].rearrange("p (h d) -> p h d", h=nh)
                t3 = t2[si][:, :].rearrange("p (h d) -> p h d", h=nh)
                o3 = ob[si][:, :].rearrange("p (h d) -> p h d", h=nh)
                rb = st_[:, 2 * nh:3 * nh].unsqueeze(2).to_broadcast([128, nh, hd])
                gbc = par[:, gcol:gcol + hd].unsqueeze(1).to_broadcast([128, nh, hd])
                B.op(dve, lambda: V.tensor_tensor(out=q3, in0=q3, in1=rb, op=ALU.mult), reads=[stb], writes=[qfb[si]])
                B.op(dve, lambda: V.tensor_tensor(out=q3, in0=q3, in1=gbc, op=ALU.mult), reads=[cbuf], writes=[qfb[si]])
                if hd == 128:
                    tab, tci = (tabA, tcx) if own else (tabAc, 0)
                else:
                    tab, tci = (tabB, tcx) if own else (tabBc, tcx)
                cosb = tab[:, 0, tci, :].unsqueeze(1).to_broadcast([128, nh, hh])
                sinb = tab[:, 1, tci, :].unsqueeze(1).to_broadcast([128, nh, hh])
                lo, hi = slice(0, hh), slice(hh, hd)
                B.op(dve, lambda: V.tensor_tensor(out=s3[:, :, lo], in0=q3[:, :, lo], in1=cosb, op=ALU.mult),
                     reads=[qfb[si], tabb], writes=[sqb[si]])
                B.op(dve, lambda: V.tensor_tensor(out=s3[:, :, hi], in0=q3[:, :, hi], in1=cosb, op=ALU.mult),
                     reads=[qfb[si], tabb], accum=[sqb[si]])
                B.op(dve, lambda: V.tensor_tensor(out=t3[:, :, lo], in0=q3[:, :, hi], in1=sinb, op=ALU.mult),
                     reads=[qfb[si], tabb], writes=[t2b[si]])
                B.op(dve, lambda: V.tensor_tensor(out=t3[:, :, hi], in0=q3[:, :, lo], in1=sinb, op=ALU.mult),
                     reads=[qfb[si], tabb], accum=[t2b[si]])
                B.op(dve, lambda: V.tensor_tensor(out=o3[:, :, lo], in0=s3[:, :, lo], in1=t3[:, :, lo], op=ALU.subtract),
                     reads=[sqb[si], t2b[si]], writes=[obb[si]])
                B.op(dve, lambda: V.tensor_tensor(out=o3[:, :, hi], in0=s3[:, :, hi], in1=t3[:, :, hi], op=ALU.add),
                     reads=[sqb[si], t2b[si]], accum=[obb[si]])
                tcl = tcx % 4
                first = (tcx == tcs[0])
                last = (tcx == tcs[-1])

                def pe_part():
                    def emit_t():
                        for cch in range(4):
                            idx = cch * 4 + tcl
                            psb = bank[tbanks[idx // 8]].bitcast(BF16)
                            ins = T.transpose(out=psb[:, (idx % 8) * 128:(idx % 8 + 1) * 128],
                                              in_=ob[si][:, cch * 128:(cch + 1) * 128], identity=ident)
                        return ins
                    tb = [bankb[tbanks[0]], bankb[tbanks[1]]]
                    if first:
                        B.op(pe, emit_t, reads=[obb[si], cbuf], writes=tb)
                    else:
                        B.op(pe, emit_t, reads=[obb[si], cbuf], accum=tb)
                    if not last:
                        return
                    ntok = 128 * len(tcs)
                    tl0 = (tcs[0] % 4) * 128
                    for hb_ in range(2):
                        psb = bank[tbanks[hb_]].bitcast(BF16)
                        B.op(act, lambda: S.copy(out=tst[ti][:, hb_ * 2:(hb_ + 1) * 2, tl0:tl0 + ntok],
                                                 in_=psb.rearrange("p (c t) -> p c t", c=2)[:, :, tl0:tl0 + ntok]),
                             reads=[bankb[tbanks[hb_]]], accum=[tstb[ti]] if hb_ else (), writes=() if hb_ else [tstb[ti]])
                    if kind == "qa":
                        cc = (c0 - C_QA) // 128
                        dap, dbuf, s0, sn = QaT[cc:cc + 4, :, tg * 512 + tl0: tg * 512 + tl0 + ntok], scr["QaT"], tl0, ntok
                    elif kind == "qb":
                        cc = (c0 - C_QB) // 128
                        dap, dbuf, s0, sn = QbT[cc:cc + 4, :, tg * 512 + tl0: tg * 512 + tl0 + ntok], scr["QbT"], tl0, ntok
                    elif kind == "kb":
                        cc = (c0 - C_KB) // 128
                        t0_ = (1024 if own else 0) + tg * 512 + tl0
                        dap, dbuf, s0, sn = KbT[cc:cc + 4, :, t0_:t0_ + ntok], scr["KbT"], tl0, ntok
                    elif own:
                        t0_ = 128 + tg * 512 + tl0
                        dap, dbuf, s0, sn = KaT[:, :, t0_:t0_ + ntok], scr["KaT"], tl0, ntok
                    else:
                        dap, dbuf, s0, sn = KaT[:, :, 0:128], scr["KaT"], 384, 128
                    B.dma(dap.rearrange("c p t -> p c t"), tst[ti][:, :, s0:s0 + sn], reads=[tstb[ti]], accum=[dbuf])
                return pe_part

            def v_epilogue(kind, c0, tcx, bi, own):
                vi = vc[0] % 2
                vc[0] += 1
                B.op(act, lambda: S.copy(out=vst[vi][:, :], in_=bank[bi]), reads=[bankb[bi]], writes=[vstb[vi]])
                if kind == "va":
                    if own:
                        dst = Va[128 + tcx * 128:128 + (tcx + 1) * 128, :]
                    elif tcx == 7:
                        dst = Va[0:128, :]
                    else:
                        return
                    db = scr["Va"]
                else:
                    r0 = (1024 if own else 0) + tcx * 128
                    dst = Vb[r0:r0 + 128, c0 - C_VB:c0 - C_VB + 512]
                    db = scr["Vb"]
                B.dma(dst, vst[vi][:, :], reads=[vstb[vi]], accum=[db])

            for bidx, (kind, c0) in enumerate(blocks):
                if pas == 0 and bidx == 1:
                    for (tc_, hf_) in ((0, 0), (0, 1), (1, 0)):
                        get_halves[1].prefetch(tc_, hf_)
                if pas == 0 and bidx == len(blocks) - 1:
                    ws.hold = ws.cursor + 4
                j0, w0, wb0 = ws.get()
                j1, w1, wb1 = ws.get()
                if bidx % 2 == 1:
                    conv_step(1)
                tcs_all = list(range(8))
                if not own and kind in ("ka", "va"):
                    tcs_all = [7]
                tokmajor_v = kind in ("va", "vb")
                for tg in range(2):
                    tcs = [t_ for t_ in tcs_all if t_ // 4 == tg]
                    if not tcs:
                        continue
                    ti, tbanks = None, None
                    if not tokmajor_v:
                        ti = tgc[0] % 2
                        tgc[0] += 1
                        tbanks = (4, 5) if ti == 0 else (6, 7)
                    for pi in range(0, len(tcs), 2):
                        pair = tcs[pi:pi + 2]
                        bs = (0, 1) if grp[0] % 2 == 0 else (2, 3)
                        grp[0] += 1

                        def emit():
                            for k in range(32):
                                w = w0 if k < 16 else w1
                                for ii, tcx in enumerate(pair):
                                    ins = T.matmul(bank[bs[ii]], lhsT=aT[:, k, tcx * 128:(tcx + 1) * 128],
                                                   rhs=w[:, k % 16, :], start=(k == 0), stop=(k == 31))
                            return ins
                        ensure_aT(pair[-1])
                        if not tab_loaded[0]:
                            tab_loaded[0] = True
                            B.dma(tabs[:, :], rope[:, :], writes=[tabb])
                        if bidx == 0 and pair[-1] == 3 and ws.hold is not None:
                            ws.gates[ws.hold] = [xload_evs[-1]]
                            ws.hold = None
                            ws.pump()
                        mm_group(emit, [aTb[t_] for t_ in pair] + [wb0, wb1], bs[:len(pair)])
                        while deferred:
                            deferred.pop(0)()
                        for ii, tcx in enumerate(pair):
                            if tokmajor_v:
                                v_epilogue(kind, c0, tcx, bs[ii], own)
                            else:
                                deferred.append(qk_epilogue(kind, c0, tg, tcs, tcx, bs[ii], ti, tbanks, own))
                ws.release(j0)
                ws.release(j1)
            while deferred:
                deferred.pop(0)()
        B.barrier()
        p1.close()

        p2 = contextlib.ExitStack()
        msk = B.sb(p2, [128, 7, 512], BF16, "msk")
        mskb = Buf()
        B.dma(msk[:, :, :], masks[:, :, :], writes=[mskb])
        esk = B.sb(p2, [128, 16, 128], F32, "esk")
        eskb = Buf()
        B.op(dve, lambda: V.tensor_copy(out=esk[:, :, :], in_=esink.unsqueeze(2).to_broadcast([128, 16, 128])),
             reads=[esb], writes=[eskb])
        pt = [B.sb(p2, [128, 1024], BF16, "pt") for _ in range(3)]
        ptb = [Buf() for _ in range(3)]
        ptc = [0]
        va_sb = B.sb(p2, [128, 9, 512], BF16, "va_sb")
        vab = Buf()
        B.dma(va_sb[:, :, :], Va[:, :].rearrange("(c p) d -> p c d", p=128), reads=[scr["Va"]], writes=[vab])
        ka_sb = [B.sb(p2, [128, 1152], BF16, "ka_sb") for _ in range(2)]
        kab = [Buf() for _ in range(2)]
        qa_sb = [B.sb(p2, [128, 4, 1024], BF16, "qa_sb") for _ in range(2)]
        qab = [Buf() for _ in range(2)]
        oa_sb = [B.sb(p2, [128, 4, 1024], BF16, "oa_sb") for _ in range(2)]
        oab = [Buf() for _ in range(2)]
        rd = [B.sb(p2, [128, 512], F32, "rd") for _ in range(2)]
        rdb = [Buf() for _ in range(2)]
        qh_sb = [B.sb(p2, [128, 1024], BF16, "qh_sb") for _ in range(2)]
        qhb = [Buf() for _ in range(2)]
        kh_sb = [B.sb(p2, [128, 2048], BF16, "kh_sb") for _ in range(2)]
        khb = [Buf() for _ in range(2)]
        vb_sb = [B.sb(p2, [128, 16, 256], BF16, "vb_sb") for _ in range(2)]
        vbb = [Buf() for _ in range(2)]
        od_sb = [B.sb(p2, [128, 1024], BF16, "od_sb") for _ in range(2)]
        odb = [Buf() for _ in range(2)]
        r1 = B.sb(p2, [128, 1024], F32, "r1"); r1b = Buf()
        u1 = B.sb(p2, [128, 512], F32, "u1"); u1b = Buf()
        u2 = B.sb(p2, [128, 512], F32, "u2"); u2b = Buf()
        sqd = B.sb(p2, [128, 512], BF16, "sqd"); sqdb = Buf()
        rs = B.sb(p2, [128, 512], F32, "rs"); rsb = Buf()
        oc = B.sb(p2, [128, 1024], F32, "oc"); ocb = Buf()

        def diff_vload(hp):
            li = hp % 2
            B.dma(vb_sb[li][:, :, :], Vb[:, hp * 256:(hp + 1) * 256].rearrange("(c p) d -> p c d", p=128),
                  reads=[scr["Vb"]], writes=[vbb[li]])

        def diff_qkload(h):
            hp, e, oi = h // 2, h % 2, h % 2
            for c in range(2):
                B.dma(qh_sb[oi][c * 64:(c + 1) * 64, :], QbT[8 * c + hp, e * 64:(e + 1) * 64, :],
                      reads=[scr["QbT"]], writes=[qhb[oi]] if c == 0 else (), accum=() if c == 0 else [qhb[oi]])
                B.dma(kh_sb[oi][c * 64:(c + 1) * 64, :], KbT[8 * c + hp, e * 64:(e + 1) * 64, :],
                      reads=[scr["KbT"]], writes=[khb[oi]] if c == 0 else (), accum=() if c == 0 else [khb[oi]])

        diff_vload(0)
        diff_qkload(0)

        it2 = [0]

        def swa_loads(kv):
            li = kv % 2
            B.dma(ka_sb[li][:, :], KaT[kv, :, :], reads=[scr["KaT"]], writes=[kab[li]])
            B.dma(qa_sb[li][:, :, :], QaT[4 * kv:4 * kv + 4, :, :].rearrange("c p t -> p c t"), reads=[scr["QaT"]], writes=[qab[li]])

        def swa_front(kv, i):
            li = kv % 2
            q_ap = qa_sb[li][:, :, i * 128:(i + 1) * 128]
            par_ = it2[0] % 2
            it2[0] += 1
            sb_ = (0, 1) if par_ == 0 else (2, 3)
            ob_ = (4, 5) if par_ == 0 else (6, 7)
            ri = par_
            mprev = msk[:, 6, :] if i == 0 else msk[:, 4, :]
            mcur = msk[:, 5, :]

            def emit_s():
                T.matmul(bank[sb_[0]].rearrange("p (g q) -> p g q", g=4), lhsT=ka_sb[li][:, i * 128:(i + 1) * 128], rhs=q_ap, start=True, stop=False)
                T.matmul(bank[sb_[0]], lhsT=ident, rhs=mprev, start=False, stop=True)
                T.matmul(bank[sb_[1]].rearrange("p (g q) -> p g q", g=4), lhsT=ka_sb[li][:, (i + 1) * 128:(i + 2) * 128], rhs=q_ap, start=True, stop=False)
                return T.matmul(bank[sb_[1]], lhsT=ident, rhs=mcur, start=False, stop=True)
            mm_group(emit_s, [kab[li], qab[li], mskb, cbuf], sb_)
            pi_ = ptc[0] % 3
            ptc[0] += 1
            B.op(act, lambda: S.activation(out=pt[pi_][:, :], in_=psum[:, sb_[0] * 512:(sb_[0] + 2) * 512],
                                           func=AF.Exp, scale=SCALE_A),
                 reads=[bankb[sb_[0]], bankb[sb_[1]]], writes=[ptb[pi_]])

            def back():
                def emit_pv():
                    T.matmul(bank[ob_[0]], lhsT=va_sb[:, i, kv * 128:(kv + 1) * 128], rhs=pt[pi_][:, 0:512], start=True, stop=False)
                    T.matmul(bank[ob_[0]], lhsT=va_sb[:, i + 1, kv * 128:(kv + 1) * 128], rhs=pt[pi_][:, 512:1024], start=False, stop=True)
                    T.matmul(bank[ob_[1]], lhsT=ones1, rhs=pt[pi_][:, 0:512], start=True, stop=False)
                    return T.matmul(bank[ob_[1]], lhsT=ones1, rhs=pt[pi_][:, 512:1024], start=False, stop=True)
                mm_group(emit_pv, [vab, ptb[pi_], cbuf], ob_)
                B.op(dve, lambda: V.tensor_tensor(out=rd[ri][:, :].rearrange("p (g q) -> p g q", g=4),
                                                  in0=bank[ob_[1]].rearrange("p (g q) -> p g q", g=4),
                                                  in1=esk[:, 4 * kv:4 * kv + 4, :], op=ALU.add),
                     reads=[bankb[ob_[1]], eskb], writes=[rdb[ri]])
                B.op(act, lambda: S.activation(out=rd[ri][:, :], in_=rd[ri][:, :], func=AF.Ln), reads=[], writes=[rdb[ri]])
                B.op(act, lambda: S.activation(out=rd[ri][:, :], in_=rd[ri][:, :], func=AF.Exp, scale=-1.0), reads=[], writes=[rdb[ri]])
                B.op(dve, lambda: V.tensor_tensor(out=oa_sb[li][:, :, i * 128:(i + 1) * 128],
                                                  in0=bank[ob_[0]].rearrange("p (g q) -> p g q", g=4),
                                                  in1=rd[ri][:, :].rearrange("p (g q) -> p g q", g=4), op=ALU.mult),
                     reads=[bankb[ob_[0]], rdb[ri]], accum=[oab[li]] if i else (), writes=() if i else [oab[li]])
                if i == 7:
                    B.dma(catT[4 * kv:4 * kv + 4, :, :].rearrange("c p t -> p c t"), oa_sb[li][:, :, :], reads=[oab[li]],
                          accum=[scr["catT"]])
            return back

        swa_loads(0)
        swa_back = None
        for kv in range(4):
            if kv + 1 < 4:
                swa_loads(kv + 1)
            for i in range(8):
                nb_ = swa_front(kv, i)
                if swa_back is not None:
                    swa_back()
                swa_back = nb_
        swa_back()

        def diff_epilogue_a(h, t):
            B.op(act, lambda: S.copy(out=oc[:, :], in_=psum[:, 4 * 512:6 * 512]), reads=[bankb[4], bankb[5]], writes=[ocb])
            B.op(act, lambda: S.activation(out=r1[:, :], in_=psum[:, 6 * 512:8 * 512], func=AF.Ln),
                 reads=[bankb[6], bankb[7]], writes=[r1b])
            B.op(act, lambda: S.activation(out=r1[:, :], in_=r1[:, :], func=AF.Exp, scale=-1.0), reads=[], writes=[r1b])
            B.op(dve, lambda: V.tensor_tensor(out=u1[:, :], in0=oc[:, 0:512], in1=r1[:, 0:512], op=ALU.mult),
                 reads=[ocb, r1b], writes=[u1b])
            B.op(dve, lambda: V.tensor_tensor(out=u2[:, :], in0=oc[:, 512:1024], in1=r1[:, 512:1024], op=ALU.mult),
                 reads=[ocb, r1b], writes=[u2b])
            B.op(dve, lambda: V.scalar_tensor_tensor(out=u1[:, :], in0=u2[:, :], scalar=neglam[:, 0:1], in1=u1[:, :],
                                                     op0=ALU.mult, op1=ALU.add), reads=[u2b, nlb], writes=[u1b])

        def diff_epilogue_b(h, t, sbank):
            oi = h % 2
            B.op(act, lambda: S.activation(out=sqd[:, :], in_=u1[:, :], func=AF.Square), reads=[u1b], writes=[sqdb])
            mm_group(lambda: T.matmul(bank[sbank], lhsT=onesn, rhs=sqd[:, :], start=True, stop=True), [sqdb, cbuf], [sbank])
            B.op(act, lambda: S.activation(out=rs[:, :], in_=bank[sbank], func=AF.Ln, bias=EPS, scale=1.0),
                 reads=[bankb[sbank]], writes=[rsb])
            B.op(act, lambda: S.activation(out=rs[:, :], in_=rs[:, :], func=AF.Exp, scale=-0.5), reads=[], writes=[rsb])
            B.op(dve, lambda: V.scalar_tensor_tensor(out=u1[:, :], in0=u1[:, :], scalar=1.0 - LAMBDA_INIT, in1=rs[:, :],
                                                     op0=ALU.mult, op1=ALU.mult), reads=[rsb], writes=[u1b])
            B.op(act, lambda: S.activation(out=od_sb[oi][:, t * 512:(t + 1) * 512], in_=u1[:, :], func=AF.Copy,
                                           scale=par[:, P_GSUB:P_GSUB + 1]),
                 reads=[u1b, cbuf], accum=[odb[oi]] if t else (), writes=() if t else [odb[oi]])
            if t == 1:
                B.dma(catT[16 + h, :, :], od_sb[oi][:, :], reads=[odb[oi]], accum=[scr["catT"]])

        pending_epi = None
        for h in range(16):
            hp, e, oi, li = h // 2, h % 2, h % 2, (h // 2) % 2
            if e == 0 and hp + 1 < 8:
                diff_vload(hp + 1)
            if h + 1 < 16:
                diff_qkload(h + 1)
            for t in range(2):
                nkb = 8 + 4 * (t + 1)
                pend = None
                for kb in range(nkb + 1):
                    if kb < nkb:
                        sbk = (0, 1) if kb % 2 == 0 else (2, 3)
                        dj = kb - 8 - 4 * t

                        def emit_s():
                            for c in range(2):
                                ins = T.matmul(bank[sbk[c]], lhsT=kh_sb[oi][c * 64:(c + 1) * 64, kb * 128:(kb + 1) * 128],
                                               rhs=qh_sb[oi][c * 64:(c + 1) * 64, t * 512:(t + 1) * 512],
                                               start=True, stop=(dj < 0))
                            if dj >= 0:
                                for c in range(2):
                                    ins = T.matmul(bank[sbk[c]], lhsT=ident, rhs=msk[:, dj, :], start=False, stop=True)
                            return ins
                        mm_group(emit_s, [khb[oi], qhb[oi], mskb, cbuf], sbk)
                        pi_ = ptc[0] % 3
                        ptc[0] += 1
                        B.op(act, lambda: S.activation(
                            out=pt[pi_][:, :], in_=psum[:, sbk[0] * 512:(sbk[0] + 2) * 512], func=AF.Exp, scale=SCALE_B),
                            reads=[bankb[sbk[0]], bankb[sbk[1]]], writes=[ptb[pi_]])
                    if kb == 3 and pending_epi is not None:
                        pending_epi(0)
                        pending_epi = None
                    if pend is not None:
                        pkb, ppi = pend

                        def emit_pv():
                            onesm = onesv if pkb < 8 else ones1
                            for c in range(2):
                                T.matmul(bank[4 + c], lhsT=vb_sb[li][:, pkb, e * 128:(e + 1) * 128],
                                         rhs=pt[ppi][:, c * 512:(c + 1) * 512], start=(pkb == 0), stop=(pkb == nkb - 1))
                                ins = T.matmul(bank[6 + c], lhsT=onesm, rhs=pt[ppi][:, c * 512:(c + 1) * 512],
                                               start=(pkb == 0), stop=(pkb == nkb - 1))
                            return ins
                        ob4 = [bankb[4], bankb[5], bankb[6], bankb[7]]
                        if pkb == 0:
                            B.op(pe, emit_pv, reads=[vbb[li], ptb[ppi], cbuf], writes=ob4)
                        else:
                            B.op(pe, emit_pv, reads=[vbb[li], ptb[ppi], cbuf], accum=ob4)
                    pend = (kb, pi_) if kb < nkb else None
                diff_epilogue_a(h, t)
                B.pool.wait_ev(pe.sem, pe.sem.n)
                conv_step(1)
                pending_epi = (lambda h=h, t=t: (lambda sbank: diff_epilogue_b(h, t, sbank)))()
        pending_epi(0)
        B.barrier()
        p2.close()

        conv_finish()
        p3 = contextlib.ExitStack()
        hres = B.sb(p3, [128, 4, D], F32, "hres")
        hb = [Buf() for _ in range(4)]
        XT = B.sb(p3, [128, 32, 512], BF16, "XT")
        XTb = Buf()
        actT = [B.sb(p3, [128, 8, 512], BF16, "actT") for _ in range(2)]
        actTb = [Buf() for _ in range(2)]
        xbf3 = B.sb(p3, [128, 4096], BF16, "xbf3")
        xbf3b = Buf()
        relu = B.sb(p3, [128, 2048], F32, "relu")
        junk3 = relu[:, 0:1024].bitcast(BF16)
        relub = Buf()
        pin = B.sb(p3, [128, 4, PLE], F32, "pin")
        pinb = Buf()
        pbf = B.sb(p3, [128, 4, PLE], BF16, "pbf")
        pbfb = Buf()
        pT = B.sb(p3, [128, 2, 512], BF16, "pT")
        pTb = Buf()
        gpl = [B.sb(p3, [128, 512], F32, "gpl") for _ in range(2)]
        gplb = [Buf() for _ in range(2)]
        sg = [B.sb(p3, [128, 512], F32, "sg") for _ in range(2)]
        sgb = [Buf() for _ in range(2)]
        peb_ = [B.sb(p3, [128, 512], F32, "peb") for _ in range(4)]
        pebb = [Buf() for _ in range(4)]
        g3 = [0]

        def nextbanks4():
            bs = (0, 1, 2, 3) if g3[0] % 2 == 0 else (4, 5, 6, 7)
            g3[0] += 1
            return bs

        def tile_loads_early(t):
            B.dma(pin[:, :, :], p_own[t * 512:(t + 1) * 512, :].rearrange("(c p) d -> p c d", p=128), writes=[pinb])
            B.dma(XT[:, :, :], catT[:, :, t * 512:(t + 1) * 512].rearrange("c p t -> p c t"), reads=[scr["catT"]], writes=[XTb])

        def tile_loads_h(t, tcs=range(4)):
            for tc in tcs:
                r0 = t * 512 + tc * 128
                B.dma(hres[:, tc, :], x_own[r0:r0 + 128, :], writes=[hb[tc]])

        tile_loads_early(0)
        tile_loads_h(0)
        for t in range(2):

            def tokmajor_block(c0):
                bs = nextbanks4()
                bb = [bankb[i] for i in bs]
                for kh in range(2):
                    j, w, wb = ws.get()

                    def emit():
                        for k in range(16):
                            for tc in range(4):
                                ins = T.matmul(bank[bs[tc]], lhsT=XT[:, kh * 16 + k, tc * 128:(tc + 1) * 128],
                                               rhs=w[:, k, :], start=(kh == 0 and k == 0), stop=(kh == 1 and k == 15))
                        return ins
                    if kh == 0:
                        B.op(pe, emit, reads=[XTb, wb], writes=bb)
                    else:
                        B.op(pe, emit, reads=[XTb, wb], accum=bb)
                    ws.release(j)
                for tc in range(4):
                    B.op(dve, lambda: V.tensor_tensor(out=hres[:, tc, c0:c0 + 512], in0=hres[:, tc, c0:c0 + 512],
                                                      in1=bank[bs[tc]], op=ALU.add),
                         reads=[bankb[bs[tc]]], accum=[hb[tc]])

            B.op(act, lambda: S.copy(out=pbf[:, :, :], in_=pin[:, :, :]), reads=[pinb], writes=[pbfb])
            psb = bank[0].bitcast(BF16)

            def emit_pt():
                for tc in range(4):
                    for kc in range(2):
                        ins = T.transpose(out=psb[:, (kc * 4 + tc) * 128:(kc * 4 + tc + 1) * 128],
                                          in_=pbf[:, tc, kc * 128:(kc + 1) * 128], identity=ident)
                return ins
            mm_group(emit_pt, [pbfb, cbuf], [0])
            B.op(act, lambda: S.copy(out=pT[:, :, :], in_=psb.rearrange("p (k t) -> p k t", k=2)), reads=[bankb[0]], writes=[pTb])

            pss = newstat(32)
            pssb = Buf()
            for cb in range(8):
                j0, w0, wb0 = ws.get()
                bs = nextbanks4()

                def emit():
                    for tc in range(4):
                        for kc in range(2):
                            ins = T.matmul(bank[bs[tc]], lhsT=pT[:, kc, tc * 128:(tc + 1) * 128], rhs=w0[:, kc, :],
                                           start=(kc == 0), stop=(kc == 1))
                    return ins
                mm_group(emit, [pTb, wb0], bs)
                ws.release(j0)
                for tc in range(4):
                    B.op(act, lambda tc=tc: S.activation(out=junk3[:, 0:512], in_=bank[bs[tc]], func=AF.Square, scale=1.0 / 64.0,
                                                         accum_out=pss[:, tc * 8 + cb: tc * 8 + cb + 1]),
                         reads=[bankb[bs[tc]]], accum=[pssb])
            prs = newstat(12)
            prsb = Buf()
            B.op(dve, lambda: V.reduce_sum(out=prs[:, 0:4], in_=pss.rearrange("p (t c) -> p t c", t=4), axis=AX.X),
                 reads=[pssb], writes=[prsb])
            B.op(act, lambda: S.activation(out=prs[:, 4:8], in_=prs[:, 0:4], func=AF.Sqrt, bias=EPS, scale=1.0), reads=[prsb], accum=[prsb])
            B.op(dve, lambda: V.reciprocal(out=prs[:, 8:12], in_=prs[:, 4:8]), reads=[prsb], accum=[prsb])

            for cb in range(8):
                tokmajor_block(cb * 512)

            def h_half(tc, hf):
                return hres[:, tc, hf * 2048:(hf + 1) * 2048], hb[tc]

            build_xT((xbf3, xbf3b, junk3), h_half, 4, P_GMLP, XT, XTb, 0, True, [(0, 1, 2, 3), (4, 5, 6, 7)])

            def up_stage(f):
                ai = f % 2
                for cbl in range(2):
                    bs = nextbanks4()
                    bb = [bankb[i] for i in bs]
                    for kh in range(2):
                        j, w, wb = ws.get()

                        def emit():
                            for k in range(16):
                                for cc in range(4):
                                    ins = T.matmul(bank[bs[cc]], lhsT=w[:, k, cc * 128:(cc + 1) * 128], rhs=XT[:, kh * 16 + k, :],
                                                   start=(kh == 0 and k == 0), stop=(kh == 1 and k == 15))
                            return ins
                        if kh == 0:
                            B.op(pe, emit, reads=[XTb, wb], writes=bb)
                        else:
                            B.op(pe, emit, reads=[XTb, wb], accum=bb)
                        ws.release(j)
                    B.op(act, lambda: S.activation(out=relu[:, :], in_=psum[:, bs[0] * 512:(bs[0] + 4) * 512], func=AF.Relu),
                         reads=bb, writes=[relub])
                    B.op(dve, lambda: V.tensor_tensor(out=actT[ai][:, cbl * 4:(cbl + 1) * 4, :],
                                                      in0=relu[:, :].rearrange("p (c t) -> p c t", c=4),
                                                      in1=relu[:, :].rearrange("p (c t) -> p c t", c=4), op=ALU.mult),
                         reads=[relub], accum=[actTb[ai]] if cbl else (), writes=() if cbl else [actTb[ai]])

            def down_stage(f):
                ai = f % 2
                for cq in range(4):
                    j0, w0, wb0 = ws.get()
                    for half in range(2):
                        c0 = cq * 1024 + half * 512
                        bs = nextbanks4()

                        def emit():
                            for k in range(8):
                                for tc in range(4):
                                    ins = T.matmul(bank[bs[tc]], lhsT=actT[ai][:, k, tc * 128:(tc + 1) * 128],
                                                   rhs=w0[:, k, half * 512:(half + 1) * 512], start=(k == 0), stop=(k == 7))
                            return ins
                        mm_group(emit, [actTb[ai], wb0], bs)
                        for tc in range(4):
                            B.op(dve, lambda tc=tc: V.tensor_tensor(out=hres[:, tc, c0:c0 + 512], in0=hres[:, tc, c0:c0 + 512],
                                                                    in1=bank[bs[tc]], op=ALU.add),
                                 reads=[bankb[bs[tc]]], accum=[hb[tc]])
                    ws.release(j0)

            up_stage(0)
            for f in range(16):
                if f + 1 < 16:
                    up_stage(f + 1)
                down_stage(f)

            build_xT((xbf3, xbf3b, junk3), h_half, 4, None, XT, XTb, 0, False, [(0, 1, 2, 3), (4, 5, 6, 7)])
            for cb in range(8):
                c0 = cb * 512
                gli = cb % 2
                B.dma(gpl[gli][:, :], gple[:, c0:c0 + 512], writes=[gplb[gli]])
                jp, wp, wpb = ws.get()
                for tp in range(2):
                    pbk = (4, 5) if tp == 0 else (6, 7)

                    def emit_p():
                        for ii in range(2):
                            tc = tp * 2 + ii
                            for kc in range(2):
                                ins = T.matmul(bank[pbk[ii]], lhsT=pT[:, kc, tc * 128:(tc + 1) * 128], rhs=wp[:, kc, :],
                                               start=(kc == 0), stop=(kc == 1))
                        return ins
                    mm_group(emit_p, [pTb, wpb], pbk)
                    for ii in range(2):
                        tc = tp * 2 + ii
                        B.op(dve, lambda: V.scalar_tensor_tensor(
                            out=peb_[tc][:, :], in0=bank[pbk[ii]], scalar=prs[:, 8 + tc: 9 + tc], in1=gpl[gli][:, :],
                            op0=ALU.mult, op1=ALU.mult), reads=[bankb[pbk[ii]], prsb, gplb[gli]], writes=[pebb[tc]])
                ws.release(jp)
                bs = (0, 1, 2, 3)
                bb = [bankb[i] for i in bs]
                for kh in range(2):
                    j, w, wb = ws.get()

                    def emit_g():
                        for k in range(16):
                            for tc in range(4):
                                ins = T.matmul(bank[bs[tc]], lhsT=XT[:, kh * 16 + k, tc * 128:(tc + 1) * 128], rhs=w[:, k, :],
                                               start=(kh == 0 and k == 0), stop=(kh == 1 and k == 15))
                        return ins
                    if kh == 0:
                        B.op(pe, emit_g, reads=[XTb, wb], writes=bb)
                    else:
                        B.op(pe, emit_g, reads=[XTb, wb], accum=bb)
                    ws.release(j)
                for tc in range(4):
                    ii = tc % 2
                    B.op(act, lambda: S.activation(out=sg[ii][:, :], in_=bank[bs[tc]], func=AF.Sigmoid),
                         reads=[bankb[bs[tc]]], writes=[sgb[ii]])
                    B.op(dve, lambda: V.tensor_tensor(out=sg[ii][:, :], in0=sg[ii][:, :], in1=peb_[tc][:, :], op=ALU.mult),
                         reads=[pebb[tc]], writes=[sgb[ii]])
                    B.op(dve, lambda: V.tensor_tensor(out=hres[:, tc, c0:c0 + 512], in0=hres[:, tc, c0:c0 + 512],
                                                      in1=sg[ii][:, :], op=ALU.add),
                         reads=[sgb[ii]], accum=[hb[tc]])
            if t + 1 < 2:
                tile_loads_early(t + 1)
            outb = Buf()
            for tc in range(4):
                r0 = t * 512 + tc * 128
                B.dma(out[r0:r0 + 128, :], hres[:, tc, :], reads=[hb[tc]], accum=[outb])
                if t + 1 < 2:
                    tile_loads_h(t + 1, [tc])
        B.barrier()
        p3.close()
    return nc


def _rope_tables(pos, dim):
    pos = pos.astype(np.float64)
    inv = 1.0 / (10000.0 ** (np.arange(0, dim, 2, dtype=np.float64) / float(dim)))
    ang = pos[:, None] * inv[None, :]
    return np.stack([np.cos(ang), np.sin(ang)], axis=0).astype(np.float32)


def _rope_pack(ra, rb):
    def lay(t):
        n = t.shape[1] // 128
        return t.reshape(2, n, 128, t.shape[2]).transpose(2, 0, 1, 3).reshape(128, -1)
    return np.ascontiguousarray(np.concatenate(
        [lay(ra[:, 0:128]), lay(rb[:, 0:1024]), lay(ra[:, 128:1152]), lay(rb[:, 1024:2048])], axis=1).astype(np.float32))


def _masks(half):
    kl = np.arange(128)[:, None]
    m = np.zeros((128, 7, 512), np.float32)
    q = np.arange(512)[None, :]
    for j in range(4):
        kg = j * 128 + kl
        m[:, j, :] = np.where(kg <= q, 0.0, NEG)
    ql = np.arange(128)[None, :]
    prev = np.where(kl > ql, 0.0, NEG)
    cur = np.where(kl <= ql, 0.0, NEG)
    m[:, 4, :] = np.tile(prev, (1, 4))
    m[:, 5, :] = np.tile(cur, (1, 4))
    m[:, 6, :] = np.tile(prev, (1, 4)) if half == 1 else NEG
    return m.astype(BF)


_CACHE = {}


def kernel(x, p, attn_norm_g, w_in, swa_q_norm_g, swa_k_norm_g, swa_sinks, diff_q_norm_g, diff_k_norm_g,
           diff_lambda_q1, diff_lambda_k1, diff_lambda_q2, diff_lambda_k2, diff_subln_g, w_o, mlp_norm_g,
           w_up, w_down, w_ple_proj, ple_norm_g, w_ple_gate, _debug=False):
    f = lambda a: np.ascontiguousarray(np.asarray(a, dtype=np.float32))
    x = f(x); p = f(p)
    w_in_, w_o_, w_up_, w_down_, w_ple_, w_gate_ = f(w_in)[0], f(w_o)[0], f(w_up)[0], f(w_down)[0], f(w_ple_proj)[0], f(w_ple_gate)[0]
    rep = lambda v, n=128: np.tile(f(v).reshape(1, -1), (n, 1))
    gple = np.ascontiguousarray(rep(ple_norm_g[0]))
    in_maps = []
    for c in range(8):
        b, half = c // 2, c % 2
        par = np.zeros((128, NPAR), np.float32)
        par[:, P_GATT:P_GATT + 32] = f(attn_norm_g)[0].reshape(32, 128).T
        par[:, P_GMLP:P_GMLP + 32] = f(mlp_norm_g)[0].reshape(32, 128).T
        par[:, P_GQA:P_GQA + 128] = rep(swa_q_norm_g[0])
        par[:, P_GKA:P_GKA + 128] = rep(swa_k_norm_g[0])
        par[:, P_GQB:P_GQB + 64] = rep(diff_q_norm_g[0])
        par[:, P_GKB:P_GKB + 64] = rep(diff_k_norm_g[0])
        par[:, P_GSUB] = f(diff_subln_g)[0]
        par[:, P_SINK:P_SINK + 16] = rep(swa_sinks[0])
        par[:, P_LQ1:P_LQ1 + 64] = rep(diff_lambda_q1[0])
        par[:, P_LK1:P_LK1 + 64] = rep(diff_lambda_k1[0])
        par[:, P_LQ2:P_LQ2 + 64] = rep(diff_lambda_q2[0])
        par[:, P_LK2:P_LK2 + 64] = rep(diff_lambda_k2[0])
        par[:, P_VALID] = float(half)
        pos0 = half * 1024
        posA = np.arange(pos0 - 128, pos0 + 1024)
        posB = np.concatenate([np.arange(0, 1024), np.arange(pos0, pos0 + 1024)])
        mats = np.zeros((128, 4, 128), np.float32)
        mats[:, 0, :] = np.eye(128)
        mats[:, 1, :] = 1.0
        mats[:, 2, :] = float(half)
        mats[:, 3, :] = 1.0 / 128.0
        in_maps.append({
            "x_own": np.ascontiguousarray(x[b, pos0:pos0 + 1024]),
            "x_ctx": np.ascontiguousarray(x[b, 0:1024]) if half == 1 else np.zeros((1024, D), np.float32),
            "p_own": np.ascontiguousarray(p[0, b, pos0:pos0 + 1024]),
            "w_in": w_in_, "w_o": w_o_, "w_up": w_up_, "w_down": w_down_, "w_ple": w_ple_, "w_gate": w_gate_,
            "params": par, "gple": gple,
            "rope": _rope_pack(_rope_tables(posA, 128), _rope_tables(posB, 64)),
            "mats": mats.astype(BF), "masks": _masks(half),
        })
    key = bool(_debug)
    if key not in _CACHE:
        _CACHE[key] = build_program(debug=_debug)
    nc = _CACHE[key]
    res = run_bass_kernel_spmd(nc, in_maps, core_ids=list(range(8)))
    outs = [r["out"] for r in res.results]
    full = np.zeros((4, 2048, D), np.float32)
    for c in range(8):
        b, half = c // 2, c % 2
        full[b, half * 1024:(half + 1) * 1024] = outs[c]
    if _debug:
        return full, res.results
    return full
```

```python
import math
import contextlib
import numpy as np
import ml_dtypes
import concourse.bass as bass
import concourse.mybir as mybir
from concourse.bass_utils import run_bass_kernel_spmd

F32 = mybir.dt.float32
BF16 = mybir.dt.bfloat16
AF = mybir.ActivationFunctionType
ALU = mybir.AluOpType
AX = mybir.AxisListType
BF = ml_dtypes.bfloat16

D = 4096
TOK = 1024
FF = 16384
PLE = 256
EPS = 1e-6
NEG = -30000.0
IN_COLS = 9216
C_QA, C_KA, C_VA, C_QB, C_KB, C_VB = 0, 2048, 2560, 3072, 5120, 7168
SCALE_A = 1.0 / math.sqrt(128.0)
SCALE_B = 1.0 / math.sqrt(64.0)
LAMBDA_INIT = 0.8 - 0.6 * math.exp(-0.3 * 0)

P_GATT, P_GMLP, P_GQA, P_GKA, P_GQB, P_GKB, P_GSUB, P_SINK, P_LQ1, P_LK1, P_LQ2, P_LK2, P_VALID = (
    0, 32, 64, 192, 320, 384, 448, 449, 465, 529, 593, 657, 721)
NPAR = 722


class Sem:
    def __init__(self, h):
        self.h = h
        self.n = 0


class Eng:
    def __init__(self, b, e, name):
        self.b = b
        self.e = e
        self.name = name
        self.sem = b.new_sem("e_" + name)
        self.waited = {}

    def wait_ev(self, s, v):
        if v <= 0 or self.waited.get(s, 0) >= v:
            return
        self.e.wait_ge(s.h, v)
        self.waited[s] = v

    def waitd(self, d):
        for s, v in d.items():
            self.wait_ev(s, v)

    def done(self, ins):
        self.sem.n += 1
        ins.then_inc(self.sem.h, 1)
        return (self.sem, self.sem.n)


class Buf:
    __slots__ = ("ready", "readers")

    def __init__(self):
        self.ready = {}
        self.readers = {}


def _merge(d, ev):
    s, v = ev
    if d.get(s, 0) < v:
        d[s] = v


class Builder:
    def __init__(self, nc, es):
        self.nc = nc
        self.es = es
        self.sems = []
        self.pe = Eng(self, nc.tensor, "pe")
        self.act = Eng(self, nc.scalar, "act")
        self.dve = Eng(self, nc.vector, "dve")
        self.sp = Eng(self, nc.sync, "sp")
        self.pool = Eng(self, nc.gpsimd, "pool")
        self.dsems = [self.new_sem("d%d" % i) for i in range(20)]
        self.di = 0
        self.uid = 0

    def new_sem(self, name):
        s = Sem(self.es.enter_context(self.nc.semaphore(name)))
        self.sems.append(s)
        return s

    def sb(self, stack, shape, dt, name=None):
        self.uid += 1
        return stack.enter_context(self.nc.sbuf_tensor("%s_%d" % (name or "t", self.uid), list(shape), dt))

    def op(self, eng, emit, reads=(), writes=(), accum=()):
        for b in reads:
            eng.waitd(b.ready)
        for b in writes:
            eng.waitd(b.readers)
            eng.waitd(b.ready)
        for b in accum:
            eng.waitd(b.readers)
            eng.waitd(b.ready)
        ins = emit()
        ev = eng.done(ins)
        for b in reads:
            _merge(b.readers, ev)
        for b in writes:
            b.ready = {ev[0]: ev[1]}
            b.readers = {}
        for b in accum:
            _merge(b.ready, ev)
        return ev

    def dma(self, out, in_, reads=(), writes=(), accum=()):
        sp = self.sp
        s = self.dsems[self.di % len(self.dsems)]
        self.di += 1
        sp.wait_ev(s, s.n)
        for b in reads:
            sp.waitd(b.ready)
        for b in writes:
            sp.waitd(b.readers)
            sp.waitd(b.ready)
        for b in accum:
            sp.waitd(b.readers)
            sp.waitd(b.ready)
        ins = self.nc.sync.dma_start(out=out, in_=in_)
        s.n += 16
        ins.then_inc(s.h, 16)
        ev = (s, s.n)
        for b in reads:
            _merge(b.readers, ev)
        for b in writes:
            b.ready = {s: s.n}
            b.readers = {}
        for b in accum:
            _merge(b.ready, ev)
        return ev

    def barrier(self):
        for e in (self.pe, self.act, self.dve, self.sp):
            for s in self.sems:
                if s in self.wsems:
                    continue
                e.wait_ev(s, s.n)


class WStream:
    def __init__(self, b, slots_main, slot_extra):
        self.b = b
        self.slot_aps = slots_main + [slot_extra]
        self.slot_buf = [Buf() for _ in self.slot_aps]
        self.sems = [b.new_sem("w%d" % i) for i in range(len(self.slot_aps))]
        b.wsems = set(self.sems)
        self.pieces = []
        self.issued = 0
        self.released = []
        self.cursor = 0
        self.hold = 2
        self.gates = {}

    def plan(self, src, nk, ncols, slot):
        self.pieces.append((src, nk, ncols, slot))
        self.released.append(False)

    def _prev_released(self, j):
        slot = self.pieces[j][3]
        for q in range(j - 1, -1, -1):
            if self.pieces[q][3] == slot:
                return self.released[q]
        return True

    def view(self, j):
        src, nk, ncols, slot = self.pieces[j]
        return self.slot_aps[slot][:, 0:nk * ncols].rearrange("p (k c) -> p k c", k=nk)

    def pump(self):
        while (self.issued < len(self.pieces) and (self.hold is None or self.issued < self.hold)
               and self._prev_released(self.issued)):
            j = self.issued
            src, nk, ncols, slot = self.pieces[j]
            buf = self.slot_buf[slot]
            pool = self.b.pool
            pool.waitd(buf.readers)
            for ev in self.gates.get(j, ()):
                pool.wait_ev(*ev)
            ins = self.b.nc.gpsimd.dma_start(out=self.view(j), in_=src)
            s = self.sems[slot]
            s.n += 16
            ins.then_inc(s.h, 16)
            buf.ready = {s: s.n}
            buf.readers = {}
            self.issued += 1

    def get(self):
        j = self.cursor
        self.cursor += 1
        self.pump()
        assert self.issued > j, "weight piece %d not issued (slot not released)" % j
        return j, self.view(j), self.slot_buf[self.pieces[j][3]]

    def release(self, j):
        self.released[j] = True
        self.pump()


def wsrc(w, r0, nk, c0, ncols):
    return w[r0:r0 + nk * 128, c0:c0 + ncols].rearrange("(k p) c -> p k c", p=128)


def build_program(debug=False):
    nc = bass.Bass("TRN2", target_bir_lowering=False)
    dram_in = lambda name, shape, dt=F32: nc.dram_tensor(name, list(shape), dt, kind="ExternalInput").ap()
    x_own = dram_in("x_own", [TOK, D])
    x_ctx = dram_in("x_ctx", [TOK, D])
    p_own = dram_in("p_own", [TOK, PLE])
    w_in = dram_in("w_in", [D, IN_COLS])
    w_o = dram_in("w_o", [D, D])
    w_up = dram_in("w_up", [D, FF])
    w_down = dram_in("w_down", [FF, D])
    w_ple = dram_in("w_ple", [PLE, D])
    w_gate = dram_in("w_gate", [D, D])
    params = dram_in("params", [128, NPAR])
    gple = dram_in("gple", [128, D])
    NTAB = 2 * 64 + 2 * 8 * 32 + 2 * 8 * 64 + 2 * 8 * 32
    rope = dram_in("rope", [128, NTAB])
    mats = dram_in("mats", [128, 4, 128], BF16)
    masks = dram_in("masks", [128, 7, 512], BF16)
    out = nc.dram_tensor("out", [TOK, D], F32, kind="ExternalOutput").ap()
    skind = dict(kind="ExternalOutput") if debug else {}
    QaT = nc.dram_tensor("QaT", [16, 128, 1024], BF16, **skind).ap()
    KaT = nc.dram_tensor("KaT", [4, 128, 1152], BF16, **skind).ap()
    Va = nc.dram_tensor("Va", [1152, 512], BF16, **skind).ap()
    QbT = nc.dram_tensor("QbT", [16, 128, 1024], BF16, **skind).ap()
    KbT = nc.dram_tensor("KbT", [16, 128, 2048], BF16, **skind).ap()
    Vb = nc.dram_tensor("Vb", [8, 128, 16, 256], BF16, **skind).ap()
    catT = nc.dram_tensor("catT", [2, 128, 32, 512], BF16, **skind).ap()
    wdown_bf = nc.dram_tensor("wdown_bf", [FF, D], BF16).ap()

    with contextlib.ExitStack() as es:
        B = Builder(nc, es)
        pe, act, dve, sp = B.pe, B.act, B.dve, B.sp
        T, V, S = nc.tensor, nc.vector, nc.scalar
        psum = es.enter_context(nc.psum_tensor("psum", [128, 4096], F32))
        bank = [psum[:, i * 512:(i + 1) * 512] for i in range(8)]
        bankb = [Buf() for _ in range(8)]

        wslots = [B.sb(es, [128, 8192], BF16, "wslot") for _ in range(3)]
        par = B.sb(es, [128, NPAR], F32, "par")
        mat = B.sb(es, [128, 4, 128], BF16, "mat")
        stat = B.sb(es, [128, 512], F32, "stat")
        neglam = B.sb(es, [128, 1], F32, "neglam")
        ident, ones1, onesv, onesn = (mat[:, i, :] for i in range(4))
        cbuf = Buf()
        statc = [32, 0]

        def newstat(n=1, persistent=False):
            if persistent:
                c = statc[1]
                statc[1] += n
                assert statc[1] <= 32
                return stat[:, c:c + n]
            if statc[0] + n > 512:
                statc[0] = 32
            c = statc[0]
            statc[0] += n
            return stat[:, c:c + n]

        B.dma(par[:, :], params[:, :], writes=[cbuf])
        B.dma(mat[:, :, :], mats[:, :, :], accum=[cbuf])

        p1 = contextlib.ExitStack()
        wextra = B.sb(p1, [128, 8192], BF16, "wextra")
        ws = WStream(B, wslots, wextra)
        blocks_ctx = [("kb", C_KB + 512 * i) for i in range(4)] + [("vb", C_VB + 512 * i) for i in range(4)] + \
                     [("ka", C_KA), ("va", C_VA)]
        blocks_own = [("qa", C_QA + 512 * i) for i in range(4)] + [("ka", C_KA), ("va", C_VA)] + \
                     [("qb", C_QB + 512 * i) for i in range(4)] + [("kb", C_KB + 512 * i) for i in range(4)] + \
                     [("vb", C_VB + 512 * i) for i in range(4)]
        n = 0
        for (kind, c0) in blocks_ctx + blocks_own:
            for kh in range(2):
                ws.plan(wsrc(w_in, kh * 2048, 16, c0, 512), 16, 512, n % 4)
                n += 1
        n = 0
        down_piece_idx = []
        for t in range(2):
            for cb in range(8):
                ws.plan(wsrc(w_ple, 0, 2, cb * 512, 512), 2, 512, n % 3); n += 1
            for cb in range(8):
                for kh in range(2):
                    ws.plan(wsrc(w_o, kh * 2048, 16, cb * 512, 512), 16, 512, n % 3); n += 1
            def up_pieces(f):
                nonlocal n
                for cbl in range(2):
                    for kh in range(2):
                        ws.plan(wsrc(w_up, kh * 2048, 16, f * 1024 + cbl * 512, 512), 16, 512, n % 3); n += 1
            def down_pieces(f):
                nonlocal n
                for cq in range(4):
                    down_piece_idx.append((len(ws.pieces), f))
                    ws.plan(wsrc(wdown_bf, f * 1024, 8, cq * 1024, 1024), 8, 1024, n % 3); n += 1
            up_pieces(0)
            for f in range(16):
                if f + 1 < 16:
                    up_pieces(f + 1)
                down_pieces(f)
            for cb in range(8):
                ws.plan(wsrc(w_ple, 0, 2, cb * 512, 512), 2, 512, n % 3); n += 1
                for kh in range(2):
                    ws.plan(wsrc(w_gate, kh * 2048, 16, cb * 512, 512), 16, 512, n % 3); n += 1

        conv_sems = [B.new_sem("cv%d" % i) for i in range(4)]
        for s_ in conv_sems:
            B.wsems.add(s_)
        conv_ev = []
        conv_next = [0]

        def conv_step(k=1):
            for _ in range(k):
                i = conv_next[0]
                if i >= 64:
                    return
                conv_next[0] += 1
                cs = conv_sems[i % 4]
                B.pool.wait_ev(cs, cs.n)
                ins = nc.gpsimd.dma_start(out=wdown_bf[i * 256:(i + 1) * 256, :], in_=w_down[i * 256:(i + 1) * 256, :])
                cs.n += 16
                ins.then_inc(cs.h, 16)
                conv_ev.append((cs, cs.n))

        def conv_finish():
            conv_step(64)
            for (pi_, f) in down_piece_idx:
                ws.gates.setdefault(pi_, []).extend(conv_ev[4 * f:4 * f + 4])

        def mm_group(emit, reads, banks_):
            return B.op(pe, emit, reads=reads, writes=[bankb[i] for i in banks_])

        def build_xT(stack_bufs, get_half, ntc, gcol0, dst, dstbuf, tok0, norm, tbanks, lazy=False):
            xbf, xbfb, junk = stack_bufs

            def chunk(tc):
                dstb = dstbuf[tc] if isinstance(dstbuf, list) else dstbuf
                halves = [get_half(tc, hf) for hf in range(2)]
                if norm:
                    ss2 = newstat(2)
                    ssb = Buf()
                    for hf in range(2):
                        ap_, hb = halves[hf]
                        B.op(act, lambda ap_=ap_, hf=hf: S.activation(
                            out=junk[:, :], in_=ap_, func=AF.Square, scale=1.0 / 64.0,
                            accum_out=ss2[:, hf:hf + 1]), reads=[hb], accum=[ssb])
                    st3 = newstat(3)
                    sb3 = Buf()
                    B.op(dve, lambda: V.tensor_tensor(out=st3[:, 0:1], in0=ss2[:, 0:1], in1=ss2[:, 1:2], op=ALU.add),
                         reads=[ssb], writes=[sb3])
                    B.op(act, lambda: S.activation(out=st3[:, 1:2], in_=st3[:, 0:1], func=AF.Sqrt, bias=EPS, scale=1.0),
                         reads=[sb3], accum=[sb3])
                    B.op(dve, lambda: V.reciprocal(out=st3[:, 2:3], in_=st3[:, 1:2]), reads=[sb3], accum=[sb3])
                    for hf in range(2):
                        ap_, hb = halves[hf]
                        B.op(dve, lambda ap_=ap_, hf=hf: V.tensor_scalar(
                            out=xbf[:, hf * 2048:(hf + 1) * 2048], in0=ap_, scalar1=st3[:, 2:3], scalar2=None,
                            op0=ALU.mult), reads=[hb, sb3], accum=[xbfb])
                else:
                    for hf in range(2):
                        ap_, hb = halves[hf]
                        B.op(act, lambda ap_=ap_, hf=hf: S.copy(out=xbf[:, hf * 2048:(hf + 1) * 2048], in_=ap_),
                             reads=[hb], accum=[xbfb])
                bs = tbanks[tc % 2]
                for q in range(4):
                    bi = bs[q]
                    psb = bank[bi].bitcast(BF16)

                    def emit(q=q, psb=psb):
                        for j in range(8):
                            k = q * 8 + j
                            ins = T.transpose(out=psb[:, j * 128:(j + 1) * 128], in_=xbf[:, k * 128:(k + 1) * 128],
                                              identity=ident)
                        return ins
                    mm_group(emit, [xbfb, cbuf], [bi])
                    o_ap = dst[:, q * 8:(q + 1) * 8, tok0 + tc * 128: tok0 + (tc + 1) * 128]
                    i_ap = psb.rearrange("p (k t) -> p k t", k=8)
                    if gcol0 is not None:
                        gb = par[:, gcol0 + q * 8: gcol0 + (q + 1) * 8].unsqueeze(2).to_broadcast([128, 8, 128])
                        B.op(dve, lambda o_ap=o_ap, i_ap=i_ap, gb=gb: V.tensor_tensor(out=o_ap, in0=i_ap, in1=gb, op=ALU.mult),
                             reads=[bankb[bi], cbuf], accum=[dstb])
                    else:
                        B.op(act, lambda o_ap=o_ap, i_ap=i_ap: S.copy(out=o_ap, in_=i_ap), reads=[bankb[bi]], accum=[dstb])

            state = [0]

            def ensure(tc_upto):
                while state[0] <= min(tc_upto, ntc - 1):
                    chunk(state[0])
                    state[0] += 1
            if lazy:
                return ensure
            ensure(ntc - 1)

        lamt = B.sb(p1, [128, 4, 64], F32, "lamt")
        lamb = Buf()
        B.op(dve, lambda: V.tensor_tensor(out=lamt[:, 0, :], in0=par[:, P_LQ1:P_LQ1 + 64], in1=par[:, P_LK1:P_LK1 + 64], op=ALU.mult),
             reads=[cbuf], accum=[lamb])
        B.op(dve, lambda: V.tensor_tensor(out=lamt[:, 1, :], in0=par[:, P_LQ2:P_LQ2 + 64], in1=par[:, P_LK2:P_LK2 + 64], op=ALU.mult),
             reads=[cbuf], accum=[lamb])
        ls = newstat(6, True)
        lsb = Buf()
        B.op(dve, lambda: V.reduce_sum(out=ls[:, 0:2], in_=lamt[:, 0:2, :], axis=AX.X), reads=[lamb], writes=[lsb])
        B.op(act, lambda: S.activation(out=ls[:, 2:4], in_=ls[:, 0:2], func=AF.Exp), reads=[lsb], accum=[lsb])
        B.op(dve, lambda: V.tensor_tensor(out=ls[:, 4:5], in0=ls[:, 3:4], in1=ls[:, 2:3], op=ALU.subtract), reads=[lsb], accum=[lsb])
        nlb = Buf()
        B.op(dve, lambda: V.tensor_scalar(out=neglam[:, :], in0=ls[:, 4:5], scalar1=-LAMBDA_INIT, scalar2=None, op0=ALU.add),
             reads=[lsb], writes=[nlb])
        esink = newstat(16, True)
        esb = Buf()
        B.op(act, lambda: S.activation(out=esink, in_=par[:, P_SINK:P_SINK + 16], func=AF.Exp), reads=[cbuf], writes=[esb])

        aT = B.sb(p1, [128, 32, 1024], BF16, "aT")
        aTb = [Buf() for _ in range(8)]
        NXST = 3
        xst = [B.sb(p1, [128, 2048], F32, "xst") for _ in range(NXST)]
        xstb = [Buf() for _ in range(NXST)]
        xbf = B.sb(p1, [128, 4096], BF16, "xbf")
        xbfb = Buf()
        junk = B.sb(p1, [128, 2048], BF16, "junk")
        tabs = B.sb(p1, [128, NTAB], F32, "tabs")
        o_ = 0
        tabAc = tabs[:, o_:o_ + 128].rearrange("p (s c d) -> p s c d", s=2, c=1); o_ += 128
        tabBc = tabs[:, o_:o_ + 512].rearrange("p (s c d) -> p s c d", s=2, c=8); o_ += 512
        tabA = tabs[:, o_:o_ + 1024].rearrange("p (s c d) -> p s c d", s=2, c=8); o_ += 1024
        tabB = tabs[:, o_:o_ + 512].rearrange("p (s c d) -> p s c d", s=2, c=8); o_ += 512
        tabb = Buf()
        tab_loaded = [False]
        NSET = 2
        qf = [B.sb(p1, [128, 512], F32, "qf") for _ in range(NSET)]
        sq = [B.sb(p1, [128, 512], F32, "sq") for _ in range(NSET)]
        t2 = [B.sb(p1, [128, 512], F32, "t2") for _ in range(NSET)]
        ob = [B.sb(p1, [128, 512], BF16, "ob") for _ in range(NSET)]
        qfb = [Buf() for _ in range(NSET)]; sqb = [Buf() for _ in range(NSET)]
        t2b = [Buf() for _ in range(NSET)]; obb = [Buf() for _ in range(NSET)]
        tst = [B.sb(p1, [128, 4, 512], BF16, "tst") for _ in range(2)]
        tstb = [Buf() for _ in range(2)]
        vst = [B.sb(p1, [128, 512], BF16, "vst") for _ in range(2)]
        vstb = [Buf() for _ in range(2)]
        scr = dict(QaT=Buf(), KaT=Buf(), Va=Buf(), QbT=Buf(), KbT=Buf(), Vb=Buf(), catT=Buf())

        hcount = [0]
        xload_evs = []

        def make_get_half(xsrc):
            cache = {}

            def issue(tc, hf):
                i = hcount[0] % NXST
                hcount[0] += 1
                xload_evs.append(B.dma(xst[i][:, :], xsrc[tc * 128:(tc + 1) * 128, hf * 2048:(hf + 1) * 2048], writes=[xstb[i]]))
                return xst[i][:, :], xstb[i]

            def get_half(tc, hf):
                if (tc, hf) in cache:
                    return cache.pop((tc, hf))
                return issue(tc, hf)

            def prefetch(tc, hf):
                cache[(tc, hf)] = issue(tc, hf)
            get_half.prefetch = prefetch
            return get_half

        get_halves = [make_get_half(x_ctx), make_get_half(x_own)]

        setc = [0]
        tgc = [0]
        vc = [0]
        grp = [0]

        for pas, (xsrc, blocks) in enumerate([(x_ctx, blocks_ctx), (x_own, blocks_own)]):
            own = pas == 1
            ensure_aT = build_xT((xbf, xbfb, junk), get_halves[pas], 8, P_GATT, aT, aTb, 0, True,
                                 [(0, 1, 2, 3), (0, 1, 2, 3)], lazy=True)

            deferred = []

            def qk_epilogue(kind, c0, tg, tcs, tcx, bi, ti, tbanks, own):
                hd = 128 if kind in ("qa", "ka") else 64
                nh = 512 // hd
                hh = hd // 2
                si = setc[0] % NSET
                setc[0] += 1
                gcol = {"qa": P_GQA, "ka": P_GKA, "qb": P_GQB, "kb": P_GKB}[kind]
                B.op(act, lambda: S.copy(out=qf[si][:, :], in_=bank[bi]), reads=[bankb[bi]], writes=[qfb[si]])
                B.op(act, lambda: S.activation(out=sq[si][:, :], in_=bank[bi], func=AF.Square),
                     reads=[bankb[bi]], writes=[sqb[si]])
                st_ = newstat(3 * nh)
                stb = Buf()
                B.op(dve, lambda: V.reduce_sum(out=st_[:, 0:nh], in_=sq[si][:, :].rearrange("p (h d) -> p h d", h=nh),
                                               axis=AX.X), reads=[sqb[si]], writes=[stb])
                B.op(act, lambda: S.activation(out=st_[:, nh:2 * nh], in_=st_[:, 0:nh], func=AF.Sqrt, bias=EPS,
                                               scale=1.0 / hd), reads=[stb], accum=[stb])
                B.op(dve, lambda: V.reciprocal(out=st_[:, 2 * nh:3 * nh], in_=st_[:, nh:2 * nh]), reads=[stb], accum=[stb])
                q3 = qf[si][:, :].rearrange("p (h d) -> p h d", h=nh)
                s3 = sq[si][:, :].rearrange("p (h d) -> p h d", h=nh)
                t3 = t2[si][:, :].rearrange("p (h d) -> p h d", h=nh)
                o3 = ob[si][:, :].rearrange("p (h d) -> p h d", h=nh)
                rb = st_[:, 2 * nh:3 * nh].unsqueeze(2).to_broadcast([128, nh, hd])
                gbc = par[:, gcol:gcol + hd].unsqueeze(1).to_broadcast([128, nh, hd])
                B.op(dve, lambda: V.tensor_tensor(out=q3, in0=q3, in1=rb, op=ALU.mult), reads=[stb], writes=[qfb[si]])
                B.op(dve, lambda: V.tensor_tensor(out=q3, in0=q3, in1=gbc, op=ALU.mult), reads=[cbuf], writes=[qfb[si]])
                if hd == 128:
                    tab, tci = (tabA, tcx) if own else (tabAc, 0)
                else:
                    tab, tci = (tabB, tcx) if own else (tabBc, tcx)
                cosb = tab[:, 0, tci, :].unsqueeze(1).to_broadcast([128, nh, hh])
                sinb = tab[:, 1, tci, :].unsqueeze(1).to_broadcast([128, nh, hh])
                lo, hi = slice(0, hh), slice(hh, hd)
                B.op(dve, lambda: V.tensor_tensor(out=s3[:, :, lo], in0=q3[:, :, lo], in1=cosb, op=ALU.mult),
                     reads=[qfb[si], tabb], writes=[sqb[si]])
                B.op(dve, lambda: V.tensor_tensor(out=s3[:, :, hi], in0=q3[:, :, hi], in1=cosb, op=ALU.mult),
                     reads=[qfb[si], tabb], accum=[sqb[si]])
                B.op(dve, lambda: V.tensor_tensor(out=t3[:, :, lo], in0=q3[:, :, hi], in1=sinb, op=ALU.mult),
                     reads=[qfb[si], tabb], writes=[t2b[si]])
                B.op(dve, lambda: V.tensor_tensor(out=t3[:, :, hi], in0=q3[:, :, lo], in1=sinb, op=ALU.mult),
                     reads=[qfb[si], tabb], accum=[t2b[si]])
                B.op(dve, lambda: V.tensor_tensor(out=o3[:, :, lo], in0=s3[:, :, lo], in1=t3[:, :, lo], op=ALU.subtract),
                     reads=[sqb[si], t2b[si]], writes=[obb[si]])
                B.op(dve, lambda: V.tensor_tensor(out=o3[:, :, hi], in0=s3[:, :, hi], in1=t3[:, :, hi], op=ALU.add),
                     reads=[sqb[si], t2b[si]], accum=[obb[si]])
                tcl = tcx % 4
                first = (tcx == tcs[0])
                last = (tcx == tcs[-1])

                def pe_part():
                    def emit_t():
                        for cch in range(4):
                            idx = cch * 4 + tcl
                            psb = bank[tbanks[idx // 8]].bitcast(BF16)
                            ins = T.transpose(out=psb[:, (idx % 8) * 128:(idx % 8 + 1) * 128],
                                              in_=ob[si][:, cch * 128:(cch + 1) * 128], identity=ident)
                        return ins
                    tb = [bankb[tbanks[0]], bankb[tbanks[1]]]
                    if first:
                        B.op(pe, emit_t, reads=[obb[si], cbuf], writes=tb)
                    else:
                        B.op(pe, emit_t, reads=[obb[si], cbuf], accum=tb)
                    if not last:
                        return
                    ntok = 128 * len(tcs)
                    tl0 = (tcs[0] % 4) * 128
                    for hb_ in range(2):
                        psb = bank[tbanks[hb_]].bitcast(BF16)
                        B.op(act, lambda: S.copy(out=tst[ti][:, hb_ * 2:(hb_ + 1) * 2, tl0:tl0 + ntok],
                                                 in_=psb.rearrange("p (c t) -> p c t", c=2)[:, :, tl0:tl0 + ntok]),
                             reads=[bankb[tbanks[hb_]]], accum=[tstb[ti]] if hb_ else (), writes=() if hb_ else [tstb[ti]])
                    if kind == "qa":
                        cc = (c0 - C_QA) // 128
                        dap, dbuf, s0, sn = QaT[cc:cc + 4, :, tg * 512 + tl0: tg * 512 + tl0 + ntok], scr["QaT"], tl0, ntok
                    elif kind == "qb":
                        cc = (c0 - C_QB) // 128
                        dap, dbuf, s0, sn = QbT[cc:cc + 4, :, tg * 512 + tl0: tg * 512 + tl0 + ntok], scr["QbT"], tl0, ntok
                    elif kind == "kb":
                        cc = (c0 - C_KB) // 128
                        t0_ = (1024 if own else 0) + tg * 512 + tl0
                        dap, dbuf, s0, sn = KbT[cc:cc + 4, :, t0_:t0_ + ntok], scr["KbT"], tl0, ntok
                    elif own:
                        t0_ = 128 + tg * 512 + tl0
                        dap, dbuf, s0, sn = KaT[:, :, t0_:t0_ + ntok], scr["KaT"], tl0, ntok
                    else:
                        dap, dbuf, s0, sn = KaT[:, :, 0:128], scr["KaT"], 384, 128
                    B.dma(dap.rearrange("c p t -> p c t"), tst[ti][:, :, s0:s0 + sn], reads=[tstb[ti]], accum=[dbuf])
                return pe_part

            def v_epilogue(kind, c0, tcx, bi, own):
                vi = vc[0] % 2
                vc[0] += 1
                B.op(act, lambda: S.copy(out=vst[vi][:, :], in_=bank[bi]), reads=[bankb[bi]], writes=[vstb[vi]])
                if kind == "va":
                    if own:
                        dst = Va[128 + tcx * 128:128 + (tcx + 1) * 128, :]
                    elif tcx == 7:
                        dst = Va[0:128, :]
                    else:
                        return
                    db = scr["Va"]
                else:
                    hp0 = (c0 - C_VB) // 256
                    cidx = (8 if own else 0) + tcx
                    B.dma(Vb[hp0:hp0 + 2, :, cidx, :].rearrange("h p d -> p h d"),
                          vst[vi][:, :].rearrange("p (h d) -> p h d", h=2), reads=[vstb[vi]], accum=[scr["Vb"]])
                    return
                B.dma(dst, vst[vi][:, :], reads=[vstb[vi]], accum=[db])

            for bidx, (kind, c0) in enumerate(blocks):
                if pas == 0 and bidx == 1:
                    for (tc_, hf_) in ((0, 0), (0, 1), (1, 0)):
                        get_halves[1].prefetch(tc_, hf_)
                j0, w0, wb0 = ws.get()
                j1, w1, wb1 = ws.get()
                conv_step(1)
                tcs_all = list(range(8))
                if not own and kind in ("ka", "va"):
                    tcs_all = [7]
                tokmajor_v = kind in ("va", "vb")
                for tg in range(2):
                    tcs = [t_ for t_ in tcs_all if t_ // 4 == tg]
                    if not tcs:
                        continue
                    ti, tbanks = None, None
                    if not tokmajor_v:
                        ti = tgc[0] % 2
                        tgc[0] += 1
                        tbanks = (4, 5) if ti == 0 else (6, 7)
                    for pi in range(0, len(tcs), 2):
                        pair = tcs[pi:pi + 2]
                        bs = (0, 1) if grp[0] % 2 == 0 else (2, 3)
                        grp[0] += 1

                        def emit():
                            for k in range(32):
                                w = w0 if k < 16 else w1
                                for ii, tcx in enumerate(pair):
                                    ins = T.matmul(bank[bs[ii]], lhsT=aT[:, k, tcx * 128:(tcx + 1) * 128],
                                                   rhs=w[:, k % 16, :], start=(k == 0), stop=(k == 31))
                            return ins
                        ensure_aT(pair[-1])
                        if not tab_loaded[0]:
                            tab_loaded[0] = True
                            B.dma(tabs[:, :], rope[:, :], writes=[tabb])
                            ws.gates[2] = [xload_evs[-1]]
                            ws.hold = None
                            ws.pump()
                        mm_group(emit, [aTb[t_] for t_ in pair] + [wb0, wb1], bs[:len(pair)])
                        while deferred:
                            deferred.pop(0)()
                        for ii, tcx in enumerate(pair):
                            if tokmajor_v:
                                v_epilogue(kind, c0, tcx, bs[ii], own)
                            else:
                                deferred.append(qk_epilogue(kind, c0, tg, tcs, tcx, bs[ii], ti, tbanks, own))
                ws.release(j0)
                ws.release(j1)
            while deferred:
                deferred.pop(0)()
        B.barrier()
        p1.close()

        p2 = contextlib.ExitStack()
        msk = B.sb(p2, [128, 7, 512], BF16, "msk")
        mskb = Buf()
        B.dma(msk[:, :, :], masks[:, :, :], writes=[mskb])
        esk = B.sb(p2, [128, 16, 128], F32, "esk")
        eskb = Buf()
        B.op(dve, lambda: V.tensor_copy(out=esk[:, :, :], in_=esink.unsqueeze(2).to_broadcast([128, 16, 128])),
             reads=[esb], writes=[eskb])
        pt = [B.sb(p2, [128, 1024], BF16, "pt") for _ in range(3)]
        ptb = [Buf() for _ in range(3)]
        ptc = [0]
        va_sb = B.sb(p2, [128, 9, 512], BF16, "va_sb")
        vab = Buf()
        B.dma(va_sb[:, :, :], Va[:, :].rearrange("(c p) d -> p c d", p=128), reads=[scr["Va"]], writes=[vab])
        ka_sb = [B.sb(p2, [128, 1152], BF16, "ka_sb") for _ in range(2)]
        kab = [Buf() for _ in range(2)]
        qa_sb = [B.sb(p2, [128, 4, 1024], BF16, "qa_sb") for _ in range(2)]
        qab = [Buf() for _ in range(2)]
        oa_sb = [B.sb(p2, [128, 4, 1024], BF16, "oa_sb") for _ in range(2)]
        oab = [Buf() for _ in range(2)]
        rd = [B.sb(p2, [128, 512], F32, "rd") for _ in range(2)]
        rdb = [Buf() for _ in range(2)]
        qh_sb = [B.sb(p2, [128, 1024], BF16, "qh_sb") for _ in range(2)]
        qhb = [Buf() for _ in range(2)]
        kh_sb = [B.sb(p2, [128, 2048], BF16, "kh_sb") for _ in range(2)]
        khb = [Buf() for _ in range(2)]
        vb_sb = [B.sb(p2, [128, 16, 256], BF16, "vb_sb") for _ in range(2)]
        vbb = [Buf() for _ in range(2)]
        od_sb = [B.sb(p2, [128, 1024], BF16, "od_sb") for _ in range(2)]
        odb = [Buf() for _ in range(2)]
        r1 = B.sb(p2, [128, 1024], F32, "r1"); r1b = Buf()
        u1 = B.sb(p2, [128, 512], F32, "u1"); u1b = Buf()
        u2 = B.sb(p2, [128, 512], F32, "u2"); u2b = Buf()
        sqd = B.sb(p2, [128, 512], BF16, "sqd"); sqdb = Buf()
        rs = B.sb(p2, [128, 512], F32, "rs"); rsb = Buf()
        oc = B.sb(p2, [128, 1024], F32, "oc"); ocb = Buf()

        def diff_vload(hp):
            li = hp % 2
            B.dma(vb_sb[li][:, :, :], Vb[hp, :, :, :], reads=[scr["Vb"]], writes=[vbb[li]])

        def diff_qkload(h):
            hp, e, oi = h // 2, h % 2, h % 2
            for c in range(2):
                B.dma(qh_sb[oi][c * 64:(c + 1) * 64, :], QbT[8 * c + hp, e * 64:(e + 1) * 64, :],
                      reads=[scr["QbT"]], writes=[qhb[oi]] if c == 0 else (), accum=() if c == 0 else [qhb[oi]])
                B.dma(kh_sb[oi][c * 64:(c + 1) * 64, :], KbT[8 * c + hp, e * 64:(e + 1) * 64, :],
                      reads=[scr["KbT"]], writes=[khb[oi]] if c == 0 else (), accum=() if c == 0 else [khb[oi]])

        diff_vload(0)
        diff_qkload(0)

        it2 = [0]

        def swa_loads(kv):
            li = kv % 2
            B.dma(ka_sb[li][:, :], KaT[kv, :, :], reads=[scr["KaT"]], writes=[kab[li]])
            B.dma(qa_sb[li][:, :, :], QaT[4 * kv:4 * kv + 4, :, :].rearrange("c p t -> p c t"), reads=[scr["QaT"]], writes=[qab[li]])

        def swa_front(kv, i):
            li = kv % 2
            q_ap = qa_sb[li][:, :, i * 128:(i + 1) * 128]
            par_ = it2[0] % 2
            it2[0] += 1
            sb_ = (0, 1) if par_ == 0 else (2, 3)
            ob_ = (4, 5) if par_ == 0 else (6, 7)
            ri = par_
            mprev = msk[:, 6, :] if i == 0 else msk[:, 4, :]
            mcur = msk[:, 5, :]

            def emit_s():
                T.matmul(bank[sb_[0]].rearrange("p (g q) -> p g q", g=4), lhsT=ka_sb[li][:, i * 128:(i + 1) * 128], rhs=q_ap, start=True, stop=False)
                T.matmul(bank[sb_[0]], lhsT=ident, rhs=mprev, start=False, stop=True)
                T.matmul(bank[sb_[1]].rearrange("p (g q) -> p g q", g=4), lhsT=ka_sb[li][:, (i + 1) * 128:(i + 2) * 128], rhs=q_ap, start=True, stop=False)
                return T.matmul(bank[sb_[1]], lhsT=ident, rhs=mcur, start=False, stop=True)
            mm_group(emit_s, [kab[li], qab[li], mskb, cbuf], sb_)
            pi_ = ptc[0] % 3
            ptc[0] += 1
            B.op(act, lambda: S.activation(out=pt[pi_][:, :], in_=psum[:, sb_[0] * 512:(sb_[0] + 2) * 512],
                                           func=AF.Exp, scale=SCALE_A),
                 reads=[bankb[sb_[0]], bankb[sb_[1]]], writes=[ptb[pi_]])

            def back():
                def emit_pv():
                    T.matmul(bank[ob_[0]], lhsT=va_sb[:, i, kv * 128:(kv + 1) * 128], rhs=pt[pi_][:, 0:512], start=True, stop=False)
                    T.matmul(bank[ob_[0]], lhsT=va_sb[:, i + 1, kv * 128:(kv + 1) * 128], rhs=pt[pi_][:, 512:1024], start=False, stop=True)
                    T.matmul(bank[ob_[1]], lhsT=ones1, rhs=pt[pi_][:, 0:512], start=True, stop=False)
                    return T.matmul(bank[ob_[1]], lhsT=ones1, rhs=pt[pi_][:, 512:1024], start=False, stop=True)
                mm_group(emit_pv, [vab, ptb[pi_], cbuf], ob_)
                B.op(dve, lambda: V.tensor_tensor(out=rd[ri][:, :].rearrange("p (g q) -> p g q", g=4),
                                                  in0=bank[ob_[1]].rearrange("p (g q) -> p g q", g=4),
                                                  in1=esk[:, 4 * kv:4 * kv + 4, :], op=ALU.add),
                     reads=[bankb[ob_[1]], eskb], writes=[rdb[ri]])
                B.op(act, lambda: S.activation(out=rd[ri][:, :], in_=rd[ri][:, :], func=AF.Ln), reads=[], writes=[rdb[ri]])
                B.op(act, lambda: S.activation(out=rd[ri][:, :], in_=rd[ri][:, :], func=AF.Exp, scale=-1.0), reads=[], writes=[rdb[ri]])
                B.op(dve, lambda: V.tensor_tensor(out=oa_sb[li][:, :, i * 128:(i + 1) * 128],
                                                  in0=bank[ob_[0]].rearrange("p (g q) -> p g q", g=4),
                                                  in1=rd[ri][:, :].rearrange("p (g q) -> p g q", g=4), op=ALU.mult),
                     reads=[bankb[ob_[0]], rdb[ri]], accum=[oab[li]] if i else (), writes=() if i else [oab[li]])
                if i == 7:
                    for t_ in range(2):
                        B.dma(catT[t_, :, 4 * kv:4 * kv + 4, :], oa_sb[li][:, :, t_ * 512:(t_ + 1) * 512], reads=[oab[li]],
                              accum=[scr["catT"]])
            return back

        swa_loads(0)
        swa_back = None
        for kv in range(4):
            if kv + 1 < 4:
                swa_loads(kv + 1)
            for i in range(8):
                nb_ = swa_front(kv, i)
                if swa_back is not None:
                    swa_back()
                swa_back = nb_
        swa_back()

        def diff_epilogue_a(h, t):
            B.op(act, lambda: S.copy(out=oc[:, :], in_=psum[:, 4 * 512:6 * 512]), reads=[bankb[4], bankb[5]], writes=[ocb])
            B.op(act, lambda: S.activation(out=r1[:, :], in_=psum[:, 6 * 512:8 * 512], func=AF.Ln),
                 reads=[bankb[6], bankb[7]], writes=[r1b])
            B.op(act, lambda: S.activation(out=r1[:, :], in_=r1[:, :], func=AF.Exp, scale=-1.0), reads=[], writes=[r1b])
            B.op(dve, lambda: V.tensor_tensor(out=u1[:, :], in0=oc[:, 0:512], in1=r1[:, 0:512], op=ALU.mult),
                 reads=[ocb, r1b], writes=[u1b])
            B.op(dve, lambda: V.tensor_tensor(out=u2[:, :], in0=oc[:, 512:1024], in1=r1[:, 512:1024], op=ALU.mult),
                 reads=[ocb, r1b], writes=[u2b])
            B.op(dve, lambda: V.scalar_tensor_tensor(out=u1[:, :], in0=u2[:, :], scalar=neglam[:, 0:1], in1=u1[:, :],
                                                     op0=ALU.mult, op1=ALU.add), reads=[u2b, nlb], writes=[u1b])

        def diff_epilogue_b(h, t, sbank):
            oi = h % 2
            B.op(act, lambda: S.activation(out=sqd[:, :], in_=u1[:, :], func=AF.Square), reads=[u1b], writes=[sqdb])
            mm_group(lambda: T.matmul(bank[sbank], lhsT=onesn, rhs=sqd[:, :], start=True, stop=True), [sqdb, cbuf], [sbank])
            B.op(act, lambda: S.activation(out=rs[:, :], in_=bank[sbank], func=AF.Ln, bias=EPS, scale=1.0),
                 reads=[bankb[sbank]], writes=[rsb])
            B.op(act, lambda: S.activation(out=rs[:, :], in_=rs[:, :], func=AF.Exp, scale=-0.5), reads=[], writes=[rsb])
            B.op(dve, lambda: V.scalar_tensor_tensor(out=u1[:, :], in0=u1[:, :], scalar=1.0 - LAMBDA_INIT, in1=rs[:, :],
                                                     op0=ALU.mult, op1=ALU.mult), reads=[rsb], writes=[u1b])
            B.op(act, lambda: S.activation(out=od_sb[oi][:, t * 512:(t + 1) * 512], in_=u1[:, :], func=AF.Copy,
                                           scale=par[:, P_GSUB:P_GSUB + 1]),
                 reads=[u1b, cbuf], accum=[odb[oi]] if t else (), writes=() if t else [odb[oi]])
            if t == 1:
                for t_ in range(2):
                    B.dma(catT[t_, :, 16 + h, :], od_sb[oi][:, t_ * 512:(t_ + 1) * 512], reads=[odb[oi]], accum=[scr["catT"]])

        pending_epi = None
        for h in range(16):
            hp, e, oi, li = h // 2, h % 2, h % 2, (h // 2) % 2
            if e == 0 and hp + 1 < 8:
                diff_vload(hp + 1)
            if h + 1 < 16:
                diff_qkload(h + 1)
            for t in range(2):
                nkb = 8 + 4 * (t + 1)
                pend = None
                for kb in range(nkb + 1):
                    if kb < nkb:
                        sbk = (0, 1) if kb % 2 == 0 else (2, 3)
                        dj = kb - 8 - 4 * t

                        def emit_s():
                            for c in range(2):
                                ins = T.matmul(bank[sbk[c]], lhsT=kh_sb[oi][c * 64:(c + 1) * 64, kb * 128:(kb + 1) * 128],
                                               rhs=qh_sb[oi][c * 64:(c + 1) * 64, t * 512:(t + 1) * 512],
                                               start=True, stop=(dj < 0))
                            if dj >= 0:
                                for c in range(2):
                                    ins = T.matmul(bank[sbk[c]], lhsT=ident, rhs=msk[:, dj, :], start=False, stop=True)
                            return ins
                        mm_group(emit_s, [khb[oi], qhb[oi], mskb, cbuf], sbk)
                        pi_ = ptc[0] % 3
                        ptc[0] += 1
                        B.op(act, lambda: S.activation(
                            out=pt[pi_][:, :], in_=psum[:, sbk[0] * 512:(sbk[0] + 2) * 512], func=AF.Exp, scale=SCALE_B),
                            reads=[bankb[sbk[0]], bankb[sbk[1]]], writes=[ptb[pi_]])
                    if kb == 3 and pending_epi is not None:
                        pending_epi(0)
                        pending_epi = None
                    if pend is not None:
                        pkb, ppi = pend

                        def emit_pv():
                            onesm = onesv if pkb < 8 else ones1
                            for c in range(2):
                                T.matmul(bank[4 + c], lhsT=vb_sb[li][:, pkb, e * 128:(e + 1) * 128],
                                         rhs=pt[ppi][:, c * 512:(c + 1) * 512], start=(pkb == 0), stop=(pkb == nkb - 1))
                                ins = T.matmul(bank[6 + c], lhsT=onesm, rhs=pt[ppi][:, c * 512:(c + 1) * 512],
                                               start=(pkb == 0), stop=(pkb == nkb - 1))
                            return ins
                        ob4 = [bankb[4], bankb[5], bankb[6], bankb[7]]
                        if pkb == 0:
                            B.op(pe, emit_pv, reads=[vbb[li], ptb[ppi], cbuf], writes=ob4)
                        else:
                            B.op(pe, emit_pv, reads=[vbb[li], ptb[ppi], cbuf], accum=ob4)
                    pend = (kb, pi_) if kb < nkb else None
                diff_epilogue_a(h, t)
                B.pool.wait_ev(pe.sem, pe.sem.n)
                conv_step(1)
                pending_epi = (lambda h=h, t=t: (lambda sbank: diff_epilogue_b(h, t, sbank)))()
        pending_epi(0)
        B.barrier()
        p2.close()

        conv_finish()
        p3 = contextlib.ExitStack()
        hres = B.sb(p3, [128, 4, D], F32, "hres")
        hb = [Buf() for _ in range(4)]
        XT = B.sb(p3, [128, 32, 512], BF16, "XT")
        XTb = Buf()
        actT = [B.sb(p3, [128, 8, 512], BF16, "actT") for _ in range(2)]
        actTb = [Buf() for _ in range(2)]
        xbf3 = B.sb(p3, [128, 4096], BF16, "xbf3")
        xbf3b = Buf()
        relu = B.sb(p3, [128, 2048], F32, "relu")
        junk3 = relu[:, 0:1024].bitcast(BF16)
        relub = Buf()
        pin = B.sb(p3, [128, 4, PLE], F32, "pin")
        pinb = Buf()
        pbf = B.sb(p3, [128, 4, PLE], BF16, "pbf")
        pbfb = Buf()
        pT = B.sb(p3, [128, 2, 512], BF16, "pT")
        pTb = Buf()
        gpl = [B.sb(p3, [128, 512], F32, "gpl") for _ in range(2)]
        gplb = [Buf() for _ in range(2)]
        sg = [B.sb(p3, [128, 512], F32, "sg") for _ in range(2)]
        sgb = [Buf() for _ in range(2)]
        peb_ = [B.sb(p3, [128, 512], F32, "peb") for _ in range(4)]
        pebb = [Buf() for _ in range(4)]
        g3 = [0]

        def nextbanks4():
            bs = (0, 1, 2, 3) if g3[0] % 2 == 0 else (4, 5, 6, 7)
            g3[0] += 1
            return bs

        def tile_loads_early(t):
            B.dma(pin[:, :, :], p_own[t * 512:(t + 1) * 512, :].rearrange("(c p) d -> p c d", p=128), writes=[pinb])
            B.dma(XT[:, :, :], catT[t, :, :, :], reads=[scr["catT"]], writes=[XTb])

        def tile_loads_h(t, tcs=range(4)):
            for tc in tcs:
                r0 = t * 512 + tc * 128
                B.dma(hres[:, tc, :], x_own[r0:r0 + 128, :], writes=[hb[tc]])

        tile_loads_early(0)
        tile_loads_h(0)
        for t in range(2):

            def tokmajor_block(c0):
                bs = nextbanks4()
                bb = [bankb[i] for i in bs]
                for kh in range(2):
                    j, w, wb = ws.get()

                    def emit():
                        for k in range(16):
                            for tc in range(4):
                                ins = T.matmul(bank[bs[tc]], lhsT=XT[:, kh * 16 + k, tc * 128:(tc + 1) * 128],
                                               rhs=w[:, k, :], start=(kh == 0 and k == 0), stop=(kh == 1 and k == 15))
                        return ins
                    if kh == 0:
                        B.op(pe, emit, reads=[XTb, wb], writes=bb)
                    else:
                        B.op(pe, emit, reads=[XTb, wb], accum=bb)
                    ws.release(j)
                for tc in range(4):
                    B.op(dve, lambda: V.tensor_tensor(out=hres[:, tc, c0:c0 + 512], in0=hres[:, tc, c0:c0 + 512],
                                                      in1=bank[bs[tc]], op=ALU.add),
                         reads=[bankb[bs[tc]]], accum=[hb[tc]])

            B.op(act, lambda: S.copy(out=pbf[:, :, :], in_=pin[:, :, :]), reads=[pinb], writes=[pbfb])
            psb = bank[0].bitcast(BF16)

            def emit_pt():
                for tc in range(4):
                    for kc in range(2):
                        ins = T.transpose(out=psb[:, (kc * 4 + tc) * 128:(kc * 4 + tc + 1) * 128],
                                          in_=pbf[:, tc, kc * 128:(kc + 1) * 128], identity=ident)
                return ins
            mm_group(emit_pt, [pbfb, cbuf], [0])
            B.op(act, lambda: S.copy(out=pT[:, :, :], in_=psb.rearrange("p (k t) -> p k t", k=2)), reads=[bankb[0]], writes=[pTb])

            pss = newstat(32)
            pssb = Buf()
            for cb in range(8):
                j0, w0, wb0 = ws.get()
                bs = nextbanks4()

                def emit():
                    for tc in range(4):
                        for kc in range(2):
                            ins = T.matmul(bank[bs[tc]], lhsT=pT[:, kc, tc * 128:(tc + 1) * 128], rhs=w0[:, kc, :],
                                           start=(kc == 0), stop=(kc == 1))
                    return ins
                mm_group(emit, [pTb, wb0], bs)
                ws.release(j0)
                for tc in range(4):
                    B.op(act, lambda tc=tc: S.activation(out=junk3[:, 0:512], in_=bank[bs[tc]], func=AF.Square, scale=1.0 / 64.0,
                                                         accum_out=pss[:, tc * 8 + cb: tc * 8 + cb + 1]),
                         reads=[bankb[bs[tc]]], accum=[pssb])
            prs = newstat(12)
            prsb = Buf()
            B.op(dve, lambda: V.reduce_sum(out=prs[:, 0:4], in_=pss.rearrange("p (t c) -> p t c", t=4), axis=AX.X),
                 reads=[pssb], writes=[prsb])
            B.op(act, lambda: S.activation(out=prs[:, 4:8], in_=prs[:, 0:4], func=AF.Sqrt, bias=EPS, scale=1.0), reads=[prsb], accum=[prsb])
            B.op(dve, lambda: V.reciprocal(out=prs[:, 8:12], in_=prs[:, 4:8]), reads=[prsb], accum=[prsb])

            for cb in range(8):
                tokmajor_block(cb * 512)

            def h_half(tc, hf):
                return hres[:, tc, hf * 2048:(hf + 1) * 2048], hb[tc]

            build_xT((xbf3, xbf3b, junk3), h_half, 4, P_GMLP, XT, XTb, 0, True, [(0, 1, 2, 3), (4, 5, 6, 7)])

            def up_stage(f):
                ai = f % 2
                for cbl in range(2):
                    bs = nextbanks4()
                    bb = [bankb[i] for i in bs]
                    for kh in range(2):
                        j, w, wb = ws.get()

                        def emit():
                            for k in range(16):
                                for cc in range(4):
                                    ins = T.matmul(bank[bs[cc]], lhsT=w[:, k, cc * 128:(cc + 1) * 128], rhs=XT[:, kh * 16 + k, :],
                                                   start=(kh == 0 and k == 0), stop=(kh == 1 and k == 15))
                            return ins
                        if kh == 0:
                            B.op(pe, emit, reads=[XTb, wb], writes=bb)
                        else:
                            B.op(pe, emit, reads=[XTb, wb], accum=bb)
                        ws.release(j)
                    B.op(act, lambda: S.activation(out=relu[:, :], in_=psum[:, bs[0] * 512:(bs[0] + 4) * 512], func=AF.Relu),
                         reads=bb, writes=[relub])
                    B.op(dve, lambda: V.tensor_tensor(out=actT[ai][:, cbl * 4:(cbl + 1) * 4, :],
                                                      in0=relu[:, :].rearrange("p (c t) -> p c t", c=4),
                                                      in1=relu[:, :].rearrange("p (c t) -> p c t", c=4), op=ALU.mult),
                         reads=[relub], accum=[actTb[ai]] if cbl else (), writes=() if cbl else [actTb[ai]])

            def down_stage(f):
                ai = f % 2
                for cq in range(4):
                    j0, w0, wb0 = ws.get()
                    for half in range(2):
                        c0 = cq * 1024 + half * 512
                        bs = nextbanks4()

                        def emit():
                            for k in range(8):
                                for tc in range(4):
                                    ins = T.matmul(bank[bs[tc]], lhsT=actT[ai][:, k, tc * 128:(tc + 1) * 128],
                                                   rhs=w0[:, k, half * 512:(half + 1) * 512], start=(k == 0), stop=(k == 7))
                            return ins
                        mm_group(emit, [actTb[ai], wb0], bs)
                        for tc in range(4):
                            B.op(dve, lambda tc=tc: V.tensor_tensor(out=hres[:, tc, c0:c0 + 512], in0=hres[:, tc, c0:c0 + 512],
                                                                    in1=bank[bs[tc]], op=ALU.add),
                                 reads=[bankb[bs[tc]]], accum=[hb[tc]])
                    ws.release(j0)

            up_stage(0)
            for f in range(16):
                if f + 1 < 16:
                    up_stage(f + 1)
                down_stage(f)

            build_xT((xbf3, xbf3b, junk3), h_half, 4, None, XT, XTb, 0, False, [(0, 1, 2, 3), (4, 5, 6, 7)])
            for cb in range(8):
                c0 = cb * 512
                gli = cb % 2
                B.dma(gpl[gli][:, :], gple[:, c0:c0 + 512], writes=[gplb[gli]])
                jp, wp, wpb = ws.get()
                for tp in range(2):
                    pbk = (4, 5) if tp == 0 else (6, 7)

                    def emit_p():
                        for ii in range(2):
                            tc = tp * 2 + ii
                            for kc in range(2):
                                ins = T.matmul(bank[pbk[ii]], lhsT=pT[:, kc, tc * 128:(tc + 1) * 128], rhs=wp[:, kc, :],
                                               start=(kc == 0), stop=(kc == 1))
                        return ins
                    mm_group(emit_p, [pTb, wpb], pbk)
                    for ii in range(2):
                        tc = tp * 2 + ii
                        B.op(dve, lambda: V.scalar_tensor_tensor(
                            out=peb_[tc][:, :], in0=bank[pbk[ii]], scalar=prs[:, 8 + tc: 9 + tc], in1=gpl[gli][:, :],
                            op0=ALU.mult, op1=ALU.mult), reads=[bankb[pbk[ii]], prsb, gplb[gli]], writes=[pebb[tc]])
                ws.release(jp)
                bs = (0, 1, 2, 3)
                bb = [bankb[i] for i in bs]
                for kh in range(2):
                    j, w, wb = ws.get()

                    def emit_g():
                        for k in range(16):
                            for tc in range(4):
                                ins = T.matmul(bank[bs[tc]], lhsT=XT[:, kh * 16 + k, tc * 128:(tc + 1) * 128], rhs=w[:, k, :],
                                               start=(kh == 0 and k == 0), stop=(kh == 1 and k == 15))
                        return ins
                    if kh == 0:
                        B.op(pe, emit_g, reads=[XTb, wb], writes=bb)
                    else:
                        B.op(pe, emit_g, reads=[XTb, wb], accum=bb)
                    ws.release(j)
                for tc in range(4):
                    ii = tc % 2
                    B.op(act, lambda: S.activation(out=sg[ii][:, :], in_=bank[bs[tc]], func=AF.Sigmoid),
                         reads=[bankb[bs[tc]]], writes=[sgb[ii]])
                    B.op(dve, lambda: V.tensor_tensor(out=sg[ii][:, :], in0=sg[ii][:, :], in1=peb_[tc][:, :], op=ALU.mult),
                         reads=[pebb[tc]], writes=[sgb[ii]])
                    B.op(dve, lambda: V.tensor_tensor(out=hres[:, tc, c0:c0 + 512], in0=hres[:, tc, c0:c0 + 512],
                                                      in1=sg[ii][:, :], op=ALU.add),
                         reads=[sgb[ii]], accum=[hb[tc]])
            if t + 1 < 2:
                tile_loads_early(t + 1)
            outb = Buf()
            for tc in range(4):
                r0 = t * 512 + tc * 128
                B.dma(out[r0:r0 + 128, :], hres[:, tc, :], reads=[hb[tc]], accum=[outb])
                if t + 1 < 2:
                    tile_loads_h(t + 1, [tc])
        B.barrier()
        p3.close()
    return nc


def _rope_tables(pos, dim):
    pos = pos.astype(np.float64)
    inv = 1.0 / (10000.0 ** (np.arange(0, dim, 2, dtype=np.float64) / float(dim)))
    ang = pos[:, None] * inv[None, :]
    return np.stack([np.cos(ang), np.sin(ang)], axis=0).astype(np.float32)


def _rope_pack(ra, rb):
    def lay(t):
        n = t.shape[1] // 128
        return t.reshape(2, n, 128, t.shape[2]).transpose(2, 0, 1, 3).reshape(128, -1)
    return np.ascontiguousarray(np.concatenate(
        [lay(ra[:, 0:128]), lay(rb[:, 0:1024]), lay(ra[:, 128:1152]), lay(rb[:, 1024:2048])], axis=1).astype(np.float32))


def _masks(half):
    kl = np.arange(128)[:, None]
    m = np.zeros((128, 7, 512), np.float32)
    q = np.arange(512)[None, :]
    for j in range(4):
        kg = j * 128 + kl
        m[:, j, :] = np.where(kg <= q, 0.0, NEG)
    ql = np.arange(128)[None, :]
    prev = np.where(kl > ql, 0.0, NEG)
    cur = np.where(kl <= ql, 0.0, NEG)
    m[:, 4, :] = np.tile(prev, (1, 4))
    m[:, 5, :] = np.tile(cur, (1, 4))
    m[:, 6, :] = np.tile(prev, (1, 4)) if half == 1 else NEG
    return m.astype(BF)


_CACHE = {}


def kernel(x, p, attn_norm_g, w_in, swa_q_norm_g, swa_k_norm_g, swa_sinks, diff_q_norm_g, diff_k_norm_g,
           diff_lambda_q1, diff_lambda_k1, diff_lambda_q2, diff_lambda_k2, diff_subln_g, w_o, mlp_norm_g,
           w_up, w_down, w_ple_proj, ple_norm_g, w_ple_gate, _debug=False):
    f = lambda a: np.ascontiguousarray(np.asarray(a, dtype=np.float32))
    x = f(x); p = f(p)
    w_in_, w_o_, w_up_, w_down_, w_ple_, w_gate_ = f(w_in)[0], f(w_o)[0], f(w_up)[0], f(w_down)[0], f(w_ple_proj)[0], f(w_ple_gate)[0]
    rep = lambda v, n=128: np.tile(f(v).reshape(1, -1), (n, 1))
    gple = np.ascontiguousarray(rep(ple_norm_g[0]))
    in_maps = []
    for c in range(8):
        b, half = c // 2, c % 2
        par = np.zeros((128, NPAR), np.float32)
        par[:, P_GATT:P_GATT + 32] = f(attn_norm_g)[0].reshape(32, 128).T
        par[:, P_GMLP:P_GMLP + 32] = f(mlp_norm_g)[0].reshape(32, 128).T
        par[:, P_GQA:P_GQA + 128] = rep(swa_q_norm_g[0])
        par[:, P_GKA:P_GKA + 128] = rep(swa_k_norm_g[0])
        par[:, P_GQB:P_GQB + 64] = rep(diff_q_norm_g[0])
        par[:, P_GKB:P_GKB + 64] = rep(diff_k_norm_g[0])
        par[:, P_GSUB] = f(diff_subln_g)[0]
        par[:, P_SINK:P_SINK + 16] = rep(swa_sinks[0])
        par[:, P_LQ1:P_LQ1 + 64] = rep(diff_lambda_q1[0])
        par[:, P_LK1:P_LK1 + 64] = rep(diff_lambda_k1[0])
        par[:, P_LQ2:P_LQ2 + 64] = rep(diff_lambda_q2[0])
        par[:, P_LK2:P_LK2 + 64] = rep(diff_lambda_k2[0])
        par[:, P_VALID] = float(half)
        pos0 = half * 1024
        posA = np.arange(pos0 - 128, pos0 + 1024)
        posB = np.concatenate([np.arange(0, 1024), np.arange(pos0, pos0 + 1024)])
        mats = np.zeros((128, 4, 128), np.float32)
        mats[:, 0, :] = np.eye(128)
        mats[:, 1, :] = 1.0
        mats[:, 2, :] = float(half)
        mats[:, 3, :] = 1.0 / 128.0
        in_maps.append({
            "x_own": np.ascontiguousarray(x[b, pos0:pos0 + 1024]),
            "x_ctx": np.ascontiguousarray(x[b, 0:1024]) if half == 1 else np.zeros((1024, D), np.float32),
            "p_own": np.ascontiguousarray(p[0, b, pos0:pos0 + 1024]),
            "w_in": w_in_, "w_o": w_o_, "w_up": w_up_, "w_down": w_down_, "w_ple": w_ple_, "w_gate": w_gate_,
            "params": par, "gple": gple,
            "rope": _rope_pack(_rope_tables(posA, 128), _rope_tables(posB, 64)),
            "mats": mats.astype(BF), "masks": _masks(half),
        })
    key = bool(_debug)
    if key not in _CACHE:
        _CACHE[key] = build_program(debug=_debug)
    nc = _CACHE[key]
    res = run_bass_kernel_spmd(nc, in_maps, core_ids=list(range(8)))
    outs = [r["out"] for r in res.results]
    full = np.zeros((4, 2048, D), np.float32)
    for c in range(8):
        b, half = c // 2, c % 2
        full[b, half * 1024:(half + 1) * 1024] = outs[c]
    if _debug:
        return full, res.results
    return full
```

```python
import math
import contextlib
import numpy as np
import ml_dtypes
import concourse.bass as bass
import concourse.mybir as mybir
from concourse.bass_utils import run_bass_kernel_spmd

F32 = mybir.dt.float32
BF16 = mybir.dt.bfloat16
AF = mybir.ActivationFunctionType
ALU = mybir.AluOpType
AX = mybir.AxisListType
BF = ml_dtypes.bfloat16

D = 4096
TOK = 1024
FF = 16384
PLE = 256
EPS = 1e-6
NEG = -30000.0
IN_COLS = 9216
C_QA, C_KA, C_VA, C_QB, C_KB, C_VB = 0, 2048, 2560, 3072, 5120, 7168
SCALE_A = 1.0 / math.sqrt(128.0)
SCALE_B = 1.0 / math.sqrt(64.0)
LAMBDA_INIT = 0.8 - 0.6 * math.exp(-0.3 * 0)

P_GATT, P_GMLP, P_GQA, P_GKA, P_GQB, P_GKB, P_GSUB, P_SINK, P_LQ1, P_LK1, P_LQ2, P_LK2, P_VALID = (
    0, 32, 64, 192, 320, 384, 448, 449, 465, 529, 593, 657, 721)
NPAR = 722


class Sem:
    def __init__(self, h):
        self.h = h
        self.n = 0


class Eng:
    def __init__(self, b, e, name):
        self.b = b
        self.e = e
        self.name = name
        self.sem = b.new_sem("e_" + name)
        self.waited = {}

    def wait_ev(self, s, v):
        if v <= 0 or self.waited.get(s, 0) >= v:
            return
        self.e.wait_ge(s.h, v)
        self.waited[s] = v

    def waitd(self, d):
        for s, v in d.items():
            self.wait_ev(s, v)

    def done(self, ins):
        self.sem.n += 1
        ins.then_inc(self.sem.h, 1)
        return (self.sem, self.sem.n)


class Buf:
    __slots__ = ("ready", "readers")

    def __init__(self):
        self.ready = {}
        self.readers = {}


def _merge(d, ev):
    s, v = ev
    if d.get(s, 0) < v:
        d[s] = v


class Builder:
    def __init__(self, nc, es):
        self.nc = nc
        self.es = es
        self.sems = []
        self.pe = Eng(self, nc.tensor, "pe")
        self.act = Eng(self, nc.scalar, "act")
        self.dve = Eng(self, nc.vector, "dve")
        self.sp = Eng(self, nc.sync, "sp")
        self.pool = Eng(self, nc.gpsimd, "pool")
        self.dsems = [self.new_sem("d%d" % i) for i in range(20)]
        self.di = 0
        self.uid = 0

    def new_sem(self, name):
        s = Sem(self.es.enter_context(self.nc.semaphore(name)))
        self.sems.append(s)
        return s

    def sb(self, stack, shape, dt, name=None):
        self.uid += 1
        return stack.enter_context(self.nc.sbuf_tensor("%s_%d" % (name or "t", self.uid), list(shape), dt))

    def op(self, eng, emit, reads=(), writes=(), accum=()):
        for b in reads:
            eng.waitd(b.ready)
        for b in writes:
            eng.waitd(b.readers)
            eng.waitd(b.ready)
        for b in accum:
            eng.waitd(b.readers)
            eng.waitd(b.ready)
        ins = emit()
        ev = eng.done(ins)
        for b in reads:
            _merge(b.readers, ev)
        for b in writes:
            b.ready = {ev[0]: ev[1]}
            b.readers = {}
        for b in accum:
            _merge(b.ready, ev)
        return ev

    def dma(self, out, in_, reads=(), writes=(), accum=()):
        sp = self.sp
        s = self.dsems[self.di % len(self.dsems)]
        self.di += 1
        sp.wait_ev(s, s.n)
        for b in reads:
            sp.waitd(b.ready)
        for b in writes:
            sp.waitd(b.readers)
            sp.waitd(b.ready)
        for b in accum:
            sp.waitd(b.readers)
            sp.waitd(b.ready)
        ins = self.nc.sync.dma_start(out=out, in_=in_)
        s.n += 16
        ins.then_inc(s.h, 16)
        ev = (s, s.n)
        for b in reads:
            _merge(b.readers, ev)
        for b in writes:
            b.ready = {s: s.n}
            b.readers = {}
        for b in accum:
            _merge(b.ready, ev)
        return ev

    def barrier(self):
        for e in (self.pe, self.act, self.dve, self.sp):
            for s in self.sems:
                if s in self.wsems:
                    continue
                e.wait_ev(s, s.n)


class WStream:
    def __init__(self, b, slots_main, slot_extra):
        self.b = b
        self.slot_aps = slots_main + [slot_extra]
        self.slot_buf = [Buf() for _ in self.slot_aps]
        self.sems = [b.new_sem("w%d" % i) for i in range(len(self.slot_aps))]
        b.wsems = set(self.sems)
        self.pieces = []
        self.issued = 0
        self.released = []
        self.cursor = 0
        self.hold = 2
        self.gates = {}

    def plan(self, src, nk, ncols, slot):
        self.pieces.append((src, nk, ncols, slot))
        self.released.append(False)

    def _prev_released(self, j):
        slot = self.pieces[j][3]
        for q in range(j - 1, -1, -1):
            if self.pieces[q][3] == slot:
                return self.released[q]
        return True

    def view(self, j):
        src, nk, ncols, slot = self.pieces[j]
        return self.slot_aps[slot][:, 0:nk * ncols].rearrange("p (k c) -> p k c", k=nk)

    def pump(self):
        while (self.issued < len(self.pieces) and (self.hold is None or self.issued < self.hold)
               and self._prev_released(self.issued)):
            j = self.issued
            src, nk, ncols, slot = self.pieces[j]
            buf = self.slot_buf[slot]
            pool = self.b.pool
            pool.waitd(buf.readers)
            for ev in self.gates.get(j, ()):
                pool.wait_ev(*ev)
            ins = self.b.nc.gpsimd.dma_start(out=self.view(j), in_=src)
            s = self.sems[slot]
            s.n += 16
            ins.then_inc(s.h, 16)
            buf.ready = {s: s.n}
            buf.readers = {}
            self.issued += 1

    def get(self):
        j = self.cursor
        self.cursor += 1
        self.pump()
        assert self.issued > j, "weight piece %d not issued (slot not released)" % j
        return j, self.view(j), self.slot_buf[self.pieces[j][3]]

    def release(self, j):
        self.released[j] = True
        self.pump()


def wsrc(w, r0, nk, c0, ncols):
    return w[r0:r0 + nk * 128, c0:c0 + ncols].rearrange("(k p) c -> p k c", p=128)


def build_program(debug=False):
    nc = bass.Bass("TRN2", target_bir_lowering=False)
    dram_in = lambda name, shape, dt=F32: nc.dram_tensor(name, list(shape), dt, kind="ExternalInput").ap()
    x_own = dram_in("x_own", [TOK, D])
    x_ctx = dram_in("x_ctx", [TOK, D])
    p_own = dram_in("p_own", [TOK, PLE])
    w_in = dram_in("w_in", [D, IN_COLS])
    w_o = dram_in("w_o", [D, D])
    w_up = dram_in("w_up", [D, FF])
    w_down = dram_in("w_down", [FF, D])
    w_ple = dram_in("w_ple", [PLE, D])
    w_gate = dram_in("w_gate", [D, D])
    params = dram_in("params", [128, NPAR])
    gple = dram_in("gple", [128, D])
    NTAB = 2 * 64 + 2 * 8 * 32 + 2 * 8 * 64 + 2 * 8 * 32
    rope = dram_in("rope", [128, NTAB])
    mats = dram_in("mats", [128, 4, 128], BF16)
    masks = dram_in("masks", [128, 7, 512], BF16)
    out = nc.dram_tensor("out", [TOK, D], F32, kind="ExternalOutput").ap()
    skind = dict(kind="ExternalOutput") if debug else {}
    QaT = nc.dram_tensor("QaT", [16, 128, 1024], BF16, **skind).ap()
    KaT = nc.dram_tensor("KaT", [4, 128, 1152], BF16, **skind).ap()
    Va = nc.dram_tensor("Va", [1152, 512], BF16, **skind).ap()
    QbT = nc.dram_tensor("QbT", [16, 128, 1024], BF16, **skind).ap()
    KbT = nc.dram_tensor("KbT", [16, 128, 2048], BF16, **skind).ap()
    Vb = nc.dram_tensor("Vb", [8, 128, 16, 256], BF16, **skind).ap()
    catT = nc.dram_tensor("catT", [2, 128, 32, 512], BF16, **skind).ap()
    wdown_bf = nc.dram_tensor("wdown_bf", [FF, D], BF16).ap()

    with contextlib.ExitStack() as es:
        B = Builder(nc, es)
        pe, act, dve, sp = B.pe, B.act, B.dve, B.sp
        T, V, S = nc.tensor, nc.vector, nc.scalar
        psum = es.enter_context(nc.psum_tensor("psum", [128, 4096], F32))
        bank = [psum[:, i * 512:(i + 1) * 512] for i in range(8)]
        bankb = [Buf() for _ in range(8)]

        wslots = [B.sb(es, [128, 8192], BF16, "wslot") for _ in range(3)]
        par = B.sb(es, [128, NPAR], F32, "par")
        mat = B.sb(es, [128, 4, 128], BF16, "mat")
        stat = B.sb(es, [128, 512], F32, "stat")
        neglam = B.sb(es, [128, 1], F32, "neglam")
        ident, ones1, onesv, onesn = (mat[:, i, :] for i in range(4))
        cbuf = Buf()
        statc = [32, 0]

        def newstat(n=1, persistent=False):
            if persistent:
                c = statc[1]
                statc[1] += n
                assert statc[1] <= 32
                return stat[:, c:c + n]
            if statc[0] + n > 512:
                statc[0] = 32
            c = statc[0]
            statc[0] += n
            return stat[:, c:c + n]

        B.dma(par[:, :], params[:, :], writes=[cbuf])
        B.dma(mat[:, :, :], mats[:, :, :], accum=[cbuf])

        p1 = contextlib.ExitStack()
        wextra = B.sb(p1, [128, 8192], BF16, "wextra")
        ws = WStream(B, wslots, wextra)
        blocks_ctx = [("kb", C_KB + 512 * i) for i in range(4)] + [("vb", C_VB + 512 * i) for i in range(4)] + \
                     [("ka", C_KA), ("va", C_VA)]
        blocks_own = [("qa", C_QA + 512 * i) for i in range(4)] + [("ka", C_KA), ("va", C_VA)] + \
                     [("qb", C_QB + 512 * i) for i in range(4)] + [("kb", C_KB + 512 * i) for i in range(4)] + \
                     [("vb", C_VB + 512 * i) for i in range(4)]
        n = 0
        for (kind, c0) in blocks_ctx + blocks_own:
            for kh in range(2):
                ws.plan(wsrc(w_in, kh * 2048, 16, c0, 512), 16, 512, n % 4)
                n += 1
        n = 0
        down_piece_idx = []
        for t in range(2):
            for cb in range(8):
                ws.plan(wsrc(w_ple, 0, 2, cb * 512, 512), 2, 512, n % 3); n += 1
            for cb in range(8):
                for kh in range(2):
                    ws.plan(wsrc(w_o, kh * 2048, 16, cb * 512, 512), 16, 512, n % 3); n += 1
            def up_pieces(f):
                nonlocal n
                for cbl in range(2):
                    for kh in range(2):
                        ws.plan(wsrc(w_up, kh * 2048, 16, f * 1024 + cbl * 512, 512), 16, 512, n % 3); n += 1
            def down_pieces(f):
                nonlocal n
                for cq in range(4):
                    down_piece_idx.append((len(ws.pieces), f))
                    ws.plan(wsrc(wdown_bf, f * 1024, 8, cq * 1024, 1024), 8, 1024, n % 3); n += 1
            up_pieces(0)
            for f in range(16):
                if f + 1 < 16:
                    up_pieces(f + 1)
                down_pieces(f)
            for cb in range(8):
                ws.plan(wsrc(w_ple, 0, 2, cb * 512, 512), 2, 512, n % 3); n += 1
                for kh in range(2):
                    ws.plan(wsrc(w_gate, kh * 2048, 16, cb * 512, 512), 16, 512, n % 3); n += 1

        conv_sems = [B.new_sem("cv%d" % i) for i in range(4)]
        for s_ in conv_sems:
            B.wsems.add(s_)
        conv_ev = []
        conv_next = [0]

        def conv_step(k=1):
            for _ in range(k):
                i = conv_next[0]
                if i >= 64:
                    return
                conv_next[0] += 1
                cs = conv_sems[i % 4]
                B.pool.wait_ev(cs, cs.n)
                ins = nc.gpsimd.dma_start(out=wdown_bf[i * 256:(i + 1) * 256, :], in_=w_down[i * 256:(i + 1) * 256, :])
                cs.n += 16
                ins.then_inc(cs.h, 16)
                conv_ev.append((cs, cs.n))

        def conv_finish():
            conv_step(64)
            for (pi_, f) in down_piece_idx:
                ws.gates.setdefault(pi_, []).extend(conv_ev[4 * f:4 * f + 4])

        def mm_group(emit, reads, banks_):
            return B.op(pe, emit, reads=reads, writes=[bankb[i] for i in banks_])

        def build_xT(stack_bufs, get_half, ntc, gcol0, dst, dstbuf, tok0, norm, tbanks, lazy=False):
            xbf, xbfb, junk = stack_bufs

            def chunk(tc):
                dstb = dstbuf[tc] if isinstance(dstbuf, list) else dstbuf
                halves = [get_half(tc, hf) for hf in range(2)]
                if norm:
                    ss2 = newstat(2)
                    ssb = Buf()
                    for hf in range(2):
                        ap_, hb = halves[hf]
                        B.op(act, lambda ap_=ap_, hf=hf: S.activation(
                            out=junk[:, :], in_=ap_, func=AF.Square, scale=1.0 / 64.0,
                            accum_out=ss2[:, hf:hf + 1]), reads=[hb], accum=[ssb])
                    st3 = newstat(3)
                    sb3 = Buf()
                    B.op(dve, lambda: V.tensor_tensor(out=st3[:, 0:1], in0=ss2[:, 0:1], in1=ss2[:, 1:2], op=ALU.add),
                         reads=[ssb], writes=[sb3])
                    B.op(act, lambda: S.activation(out=st3[:, 1:2], in_=st3[:, 0:1], func=AF.Sqrt, bias=EPS, scale=1.0),
                         reads=[sb3], accum=[sb3])
                    B.op(dve, lambda: V.reciprocal(out=st3[:, 2:3], in_=st3[:, 1:2]), reads=[sb3], accum=[sb3])
                    for hf in range(2):
                        ap_, hb = halves[hf]
                        B.op(dve, lambda ap_=ap_, hf=hf: V.tensor_scalar(
                            out=xbf[:, hf * 2048:(hf + 1) * 2048], in0=ap_, scalar1=st3[:, 2:3], scalar2=None,
                            op0=ALU.mult), reads=[hb, sb3], accum=[xbfb])
                else:
                    for hf in range(2):
                        ap_, hb = halves[hf]
                        B.op(act, lambda ap_=ap_, hf=hf: S.copy(out=xbf[:, hf * 2048:(hf + 1) * 2048], in_=ap_),
                             reads=[hb], accum=[xbfb])
                bs = tbanks[tc % 2]
                for q in range(4):
                    bi = bs[q]
                    psb = bank[bi].bitcast(BF16)

                    def emit(q=q, psb=psb):
                        for j in range(8):
                            k = q * 8 + j
                            ins = T.transpose(out=psb[:, j * 128:(j + 1) * 128], in_=xbf[:, k * 128:(k + 1) * 128],
                                              identity=ident)
                        return ins
                    mm_group(emit, [xbfb, cbuf], [bi])
                    o_ap = dst[:, q * 8:(q + 1) * 8, tok0 + tc * 128: tok0 + (tc + 1) * 128]
                    i_ap = psb.rearrange("p (k t) -> p k t", k=8)
                    if gcol0 is not None:
                        gb = par[:, gcol0 + q * 8: gcol0 + (q + 1) * 8].unsqueeze(2).to_broadcast([128, 8, 128])
                        B.op(dve, lambda o_ap=o_ap, i_ap=i_ap, gb=gb: V.tensor_tensor(out=o_ap, in0=i_ap, in1=gb, op=ALU.mult),
                             reads=[bankb[bi], cbuf], accum=[dstb])
                    else:
                        B.op(act, lambda o_ap=o_ap, i_ap=i_ap: S.copy(out=o_ap, in_=i_ap), reads=[bankb[bi]], accum=[dstb])

            state = [0]

            def ensure(tc_upto):
                while state[0] <= min(tc_upto, ntc - 1):
                    chunk(state[0])
                    state[0] += 1
            if lazy:
                return ensure
            ensure(ntc - 1)

        lamt = B.sb(p1, [128, 4, 64], F32, "lamt")
        lamb = Buf()
        B.op(dve, lambda: V.tensor_tensor(out=lamt[:, 0, :], in0=par[:, P_LQ1:P_LQ1 + 64], in1=par[:, P_LK1:P_LK1 + 64], op=ALU.mult),
             reads=[cbuf], accum=[lamb])
        B.op(dve, lambda: V.tensor_tensor(out=lamt[:, 1, :], in0=par[:, P_LQ2:P_LQ2 + 64], in1=par[:, P_LK2:P_LK2 + 64], op=ALU.mult),
             reads=[cbuf], accum=[lamb])
        ls = newstat(6, True)
        lsb = Buf()
        B.op(dve, lambda: V.reduce_sum(out=ls[:, 0:2], in_=lamt[:, 0:2, :], axis=AX.X), reads=[lamb], writes=[lsb])
        B.op(act, lambda: S.activation(out=ls[:, 2:4], in_=ls[:, 0:2], func=AF.Exp), reads=[lsb], accum=[lsb])
        B.op(dve, lambda: V.tensor_tensor(out=ls[:, 4:5], in0=ls[:, 3:4], in1=ls[:, 2:3], op=ALU.subtract), reads=[lsb], accum=[lsb])
        nlb = Buf()
        B.op(dve, lambda: V.tensor_scalar(out=neglam[:, :], in0=ls[:, 4:5], scalar1=-LAMBDA_INIT, scalar2=None, op0=ALU.add),
             reads=[lsb], writes=[nlb])
        esink = newstat(16, True)
        esb = Buf()
        B.op(act, lambda: S.activation(out=esink, in_=par[:, P_SINK:P_SINK + 16], func=AF.Exp), reads=[cbuf], writes=[esb])

        aT = B.sb(p1, [128, 32, 1024], BF16, "aT")
        aTb = [Buf() for _ in range(8)]
        NXST = 3
        xst = [B.sb(p1, [128, 2048], F32, "xst") for _ in range(NXST)]
        xstb = [Buf() for _ in range(NXST)]
        xbf = B.sb(p1, [128, 4096], BF16, "xbf")
        xbfb = Buf()
        junk = B.sb(p1, [128, 2048], BF16, "junk")
        tabs = B.sb(p1, [128, NTAB], F32, "tabs")
        o_ = 0
        tabAc = tabs[:, o_:o_ + 128].rearrange("p (s c d) -> p s c d", s=2, c=1); o_ += 128
        tabBc = tabs[:, o_:o_ + 512].rearrange("p (s c d) -> p s c d", s=2, c=8); o_ += 512
        tabA = tabs[:, o_:o_ + 1024].rearrange("p (s c d) -> p s c d", s=2, c=8); o_ += 1024
        tabB = tabs[:, o_:o_ + 512].rearrange("p (s c d) -> p s c d", s=2, c=8); o_ += 512
        tabb = Buf()
        tab_loaded = [False]
        NSET = 2
        qf = [B.sb(p1, [128, 512], F32, "qf") for _ in range(NSET)]
        sq = [B.sb(p1, [128, 512], F32, "sq") for _ in range(NSET)]
        t2 = [B.sb(p1, [128, 512], F32, "t2") for _ in range(NSET)]
        ob = [B.sb(p1, [128, 512], BF16, "ob") for _ in range(NSET)]
        qfb = [Buf() for _ in range(NSET)]; sqb = [Buf() for _ in range(NSET)]
        t2b = [Buf() for _ in range(NSET)]; obb = [Buf() for _ in range(NSET)]
        tst = [B.sb(p1, [128, 4, 512], BF16, "tst") for _ in range(2)]
        tstb = [Buf() for _ in range(2)]
        vst = [B.sb(p1, [128, 512], BF16, "vst") for _ in range(2)]
        vstb = [Buf() for _ in range(2)]
        scr = dict(QaT=Buf(), KaT=Buf(), Va=Buf(), QbT=Buf(), KbT=Buf(), Vb=Buf(), catT=Buf())

        hcount = [0]
        xload_evs = []

        def make_get_half(xsrc):
            cache = {}

            def issue(tc, hf):
                i = hcount[0] % NXST
                hcount[0] += 1
                xload_evs.append(B.dma(xst[i][:, :], xsrc[tc * 128:(tc + 1) * 128, hf * 2048:(hf + 1) * 2048], writes=[xstb[i]]))
                return xst[i][:, :], xstb[i]

            def get_half(tc, hf):
                if (tc, hf) in cache:
                    return cache.pop((tc, hf))
                return issue(tc, hf)

            def prefetch(tc, hf):
                cache[(tc, hf)] = issue(tc, hf)
            get_half.prefetch = prefetch
            return get_half

        get_halves = [make_get_half(x_ctx), make_get_half(x_own)]

        setc = [0]
        tgc = [0]
        vc = [0]
        grp = [0]

        for pas, (xsrc, blocks) in enumerate([(x_ctx, blocks_ctx), (x_own, blocks_own)]):
            own = pas == 1
            ensure_aT = build_xT((xbf, xbfb, junk), get_halves[pas], 8, P_GATT, aT, aTb, 0, True,
                                 [(0, 1, 2, 3), (0, 1, 2, 3)], lazy=True)

            deferred = []

            def qk_epilogue(kind, c0, tg, tcs, tcx, bi, ti, tbanks, own):
                hd = 128 if kind in ("qa", "ka") else 64
                nh = 512 // hd
                hh = hd // 2
                si = setc[0] % NSET
                setc[0] += 1
                gcol = {"qa": P_GQA, "ka": P_GKA, "qb": P_GQB, "kb": P_GKB}[kind]
                B.op(act, lambda: S.copy(out=qf[si][:, :], in_=bank[bi]), reads=[bankb[bi]], writes=[qfb[si]])
                B.op(act, lambda: S.activation(out=sq[si][:, :], in_=bank[bi], func=AF.Square),
                     reads=[bankb[bi]], writes=[sqb[si]])
                st_ = newstat(3 * nh)
                stb = Buf()
                B.op(dve, lambda: V.reduce_sum(out=st_[:, 0:nh], in_=sq[si][:, :].rearrange("p (h d) -> p h d", h=nh),
                                               axis=AX.X), reads=[sqb[si]], writes=[stb])
                B.op(act, lambda: S.activation(out=st_[:, nh:2 * nh], in_=st_[:, 0:nh], func=AF.Sqrt, bias=EPS,
                                               scale=1.0 / hd), reads=[stb], accum=[stb])
                B.op(dve, lambda: V.reciprocal(out=st_[:, 2 * nh:3 * nh], in_=st_[:, nh:2 * nh]), reads=[stb], accum=[stb])
                q3 = qf[si][:, :].rearrange("p (h d) -> p h d", h=nh)
                s3 = sq[si][:, :].rearrange("p (h d) -> p h d", h=nh)
                t3 = t2[si][:, :].rearrange("p (h d) -> p h d", h=nh)
                o3 = ob[si][:, :].rearrange("p (h d) -> p h d", h=nh)
                rb = st_[:, 2 * nh:3 * nh].unsqueeze(2).to_broadcast([128, nh, hd])
                gbc = par[:, gcol:gcol + hd].unsqueeze(1).to_broadcast([128, nh, hd])
                B.op(dve, lambda: V.tensor_tensor(out=q3, in0=q3, in1=rb, op=ALU.mult), reads=[stb], writes=[qfb[si]])
                B.op(dve, lambda: V.tensor_tensor(out=q3, in0=q3, in1=gbc, op=ALU.mult), reads=[cbuf], writes=[qfb[si]])
                if hd == 128:
                    tab, tci = (tabA, tcx) if own else (tabAc, 0)
                else:
                    tab, tci = (tabB, tcx) if own else (tabBc, tcx)
                cosb = tab[:, 0, tci, :].unsqueeze(1).to_broadcast([128, nh, hh])
                sinb = tab[:, 1, tci, :].unsqueeze(1).to_broadcast([128, nh, hh])
                lo, hi = slice(0, hh), slice(hh, hd)
                B.op(dve, lambda: V.tensor_tensor(out=s3[:, :, lo], in0=q3[:, :, lo], in1=cosb, op=ALU.mult),
                     reads=[qfb[si], tabb], writes=[sqb[si]])
                B.op(dve, lambda: V.tensor_tensor(out=s3[:, :, hi], in0=q3[:, :, hi], in1=cosb, op=ALU.mult),
                     reads=[qfb[si], tabb], accum=[sqb[si]])
                B.op(dve, lambda: V.tensor_tensor(out=t3[:, :, lo], in0=q3[:, :, hi], in1=sinb, op=ALU.mult),
                     reads=[qfb[si], tabb], writes=[t2b[si]])
                B.op(dve, lambda: V.tensor_tensor(out=t3[:, :, hi], in0=q3[:, :, lo], in1=sinb, op=ALU.mult),
                     reads=[qfb[si], tabb], accum=[t2b[si]])
                B.op(dve, lambda: V.tensor_tensor(out=o3[:, :, lo], in0=s3[:, :, lo], in1=t3[:, :, lo], op=ALU.subtract),
                     reads=[sqb[si], t2b[si]], writes=[obb[si]])
                B.op(dve, lambda: V.tensor_tensor(out=o3[:, :, hi], in0=s3[:, :, hi], in1=t3[:, :, hi], op=ALU.add),
                     reads=[sqb[si], t2b[si]], accum=[obb[si]])
                tcl = tcx % 4
                first = (tcx == tcs[0])
                last = (tcx == tcs[-1])

                def pe_part():
                    def emit_t():
                        for cch in range(4):
                            idx = cch * 4 + tcl
                            psb = bank[tbanks[idx // 8]].bitcast(BF16)
                            ins = T.transpose(out=psb[:, (idx % 8) * 128:(idx % 8 + 1) * 128],
                                              in_=ob[si][:, cch * 128:(cch + 1) * 128], identity=ident)
                        return ins
                    tb = [bankb[tbanks[0]], bankb[tbanks[1]]]
                    if first:
                        B.op(pe, emit_t, reads=[obb[si], cbuf], writes=tb)
                    else:
                        B.op(pe, emit_t, reads=[obb[si], cbuf], accum=tb)
                    if not last:
                        return
                    ntok = 128 * len(tcs)
                    tl0 = (tcs[0] % 4) * 128
                    for hb_ in range(2):
                        psb = bank[tbanks[hb_]].bitcast(BF16)
                        B.op(act, lambda: S.copy(out=tst[ti][:, hb_ * 2:(hb_ + 1) * 2, tl0:tl0 + ntok],
                                                 in_=psb.rearrange("p (c t) -> p c t", c=2)[:, :, tl0:tl0 + ntok]),
                             reads=[bankb[tbanks[hb_]]], accum=[tstb[ti]] if hb_ else (), writes=() if hb_ else [tstb[ti]])
                    if kind == "qa":
                        cc = (c0 - C_QA) // 128
                        dap, dbuf, s0, sn = QaT[cc:cc + 4, :, tg * 512 + tl0: tg * 512 + tl0 + ntok], scr["QaT"], tl0, ntok
                    elif kind == "qb":
                        cc = (c0 - C_QB) // 128
                        dap, dbuf, s0, sn = QbT[cc:cc + 4, :, tg * 512 + tl0: tg * 512 + tl0 + ntok], scr["QbT"], tl0, ntok
                    elif kind == "kb":
                        cc = (c0 - C_KB) // 128
                        t0_ = (1024 if own else 0) + tg * 512 + tl0
                        dap, dbuf, s0, sn = KbT[cc:cc + 4, :, t0_:t0_ + ntok], scr["KbT"], tl0, ntok
                    elif own:
                        t0_ = 128 + tg * 512 + tl0
                        dap, dbuf, s0, sn = KaT[:, :, t0_:t0_ + ntok], scr["KaT"], tl0, ntok
                    else:
                        dap, dbuf, s0, sn = KaT[:, :, 0:128], scr["KaT"], 384, 128
                    B.dma(dap.rearrange("c p t -> p c t"), tst[ti][:, :, s0:s0 + sn], reads=[tstb[ti]], accum=[dbuf])
                return pe_part

            def v_epilogue(kind, c0, tcx, bi, own):
                vi = vc[0] % 2
                vc[0] += 1
                B.op(act, lambda: S.copy(out=vst[vi][:, :], in_=bank[bi]), reads=[bankb[bi]], writes=[vstb[vi]])
                if kind == "va":
                    if own:
                        dst = Va[128 + tcx * 128:128 + (tcx + 1) * 128, :]
                    elif tcx == 7:
                        dst = Va[0:128, :]
                    else:
                        return
                    db = scr["Va"]
                else:
                    hp0 = (c0 - C_VB) // 256
                    cidx = (8 if own else 0) + tcx
                    B.dma(Vb[hp0:hp0 + 2, :, cidx, :].rearrange("h p d -> p h d"),
                          vst[vi][:, :].rearrange("p (h d) -> p h d", h=2), reads=[vstb[vi]], accum=[scr["Vb"]])
                    return
                B.dma(dst, vst[vi][:, :], reads=[vstb[vi]], accum=[db])

            for bidx, (kind, c0) in enumerate(blocks):
                if pas == 0 and bidx == 1:
                    for (tc_, hf_) in ((0, 0), (0, 1), (1, 0)):
                        get_halves[1].prefetch(tc_, hf_)
                j0, w0, wb0 = ws.get()
                j1, w1, wb1 = ws.get()
                conv_step(1)
                tcs_all = list(range(8))
                if not own and kind in ("ka", "va"):
                    tcs_all = [7]
                tokmajor_v = kind in ("va", "vb")
                for tg in range(2):
                    tcs = [t_ for t_ in tcs_all if t_ // 4 == tg]
                    if not tcs:
                        continue
                    ti, tbanks = None, None
                    if not tokmajor_v:
                        ti = tgc[0] % 2
                        tgc[0] += 1
                        tbanks = (4, 5) if ti == 0 else (6, 7)
                    for pi in range(0, len(tcs), 2):
                        pair = tcs[pi:pi + 2]
                        bs = (0, 1) if grp[0] % 2 == 0 else (2, 3)
                        grp[0] += 1

                        def emit():
                            for k in range(32):
                                w = w0 if k < 16 else w1
                                for ii, tcx in enumerate(pair):
                                    ins = T.matmul(bank[bs[ii]], lhsT=aT[:, k, tcx * 128:(tcx + 1) * 128],
                                                   rhs=w[:, k % 16, :], start=(k == 0), stop=(k == 31))
                            return ins
                        ensure_aT(pair[-1])
                        if not tab_loaded[0]:
                            tab_loaded[0] = True
                            B.dma(tabs[:, :], rope[:, :], writes=[tabb])
                            ws.gates[2] = [xload_evs[-1]]
                            ws.hold = None
                            ws.pump()
                        mm_group(emit, [aTb[t_] for t_ in pair] + [wb0, wb1], bs[:len(pair)])
                        while deferred:
                            deferred.pop(0)()
                        for ii, tcx in enumerate(pair):
                            if tokmajor_v:
                                v_epilogue(kind, c0, tcx, bs[ii], own)
                            else:
                                deferred.append(qk_epilogue(kind, c0, tg, tcs, tcx, bs[ii], ti, tbanks, own))
                ws.release(j0)
                ws.release(j1)
            while deferred:
                deferred.pop(0)()
        B.barrier()
        p1.close()

        p2 = contextlib.ExitStack()
        msk = B.sb(p2, [128, 7, 512], BF16, "msk")
        mskb = Buf()
        B.dma(msk[:, :, :], masks[:, :, :], writes=[mskb])
        esk = B.sb(p2, [128, 16, 128], F32, "esk")
        eskb = Buf()
        B.op(dve, lambda: V.tensor_copy(out=esk[:, :, :], in_=esink.unsqueeze(2).to_broadcast([128, 16, 128])),
             reads=[esb], writes=[eskb])
        pt = [B.sb(p2, [128, 1024], BF16, "pt") for _ in range(3)]
        ptb = [Buf() for _ in range(3)]
        ptc = [0]
        va_sb = B.sb(p2, [128, 9, 512], BF16, "va_sb")
        vab = Buf()
        B.dma(va_sb[:, :, :], Va[:, :].rearrange("(c p) d -> p c d", p=128), reads=[scr["Va"]], writes=[vab])
        ka_sb = [B.sb(p2, [128, 1152], BF16, "ka_sb") for _ in range(2)]
        kab = [Buf() for _ in range(2)]
        qa_sb = [B.sb(p2, [128, 4, 1024], BF16, "qa_sb") for _ in range(2)]
        qab = [Buf() for _ in range(2)]
        oa_sb = [B.sb(p2, [128, 4, 1024], BF16, "oa_sb") for _ in range(2)]
        oab = [Buf() for _ in range(2)]
        rd = [B.sb(p2, [128, 512], F32, "rd") for _ in range(2)]
        rdb = [Buf() for _ in range(2)]
        qh_sb = [B.sb(p2, [128, 1024], BF16, "qh_sb") for _ in range(2)]
        qhb = [Buf() for _ in range(2)]
        kh_sb = [B.sb(p2, [128, 2048], BF16, "kh_sb") for _ in range(2)]
        khb = [Buf() for _ in range(2)]
        vb_sb = [B.sb(p2, [128, 16, 256], BF16, "vb_sb") for _ in range(2)]
        vbb = [Buf() for _ in range(2)]
        od_sb = [B.sb(p2, [128, 1024], BF16, "od_sb") for _ in range(2)]
        odb = [Buf() for _ in range(2)]
        r1 = B.sb(p2, [128, 1024], F32, "r1"); r1b = Buf()
        u1 = B.sb(p2, [128, 512], F32, "u1"); u1b = Buf()
        u2 = B.sb(p2, [128, 512], F32, "u2"); u2b = Buf()
        sqd = B.sb(p2, [128, 512], BF16, "sqd"); sqdb = Buf()
        rs = B.sb(p2, [128, 512], F32, "rs"); rsb = Buf()
        oc = B.sb(p2, [128, 1024], F32, "oc"); ocb = Buf()

        def diff_vload(hp):
            li = hp % 2
            B.dma(vb_sb[li][:, :, :], Vb[hp, :, :, :], reads=[scr["Vb"]], writes=[vbb[li]])

        def diff_qkload(h):
            hp, e, oi = h // 2, h % 2, h % 2
            for c in range(2):
                B.dma(qh_sb[oi][c * 64:(c + 1) * 64, :], QbT[8 * c + hp, e * 64:(e + 1) * 64, :],
                      reads=[scr["QbT"]], writes=[qhb[oi]] if c == 0 else (), accum=() if c == 0 else [qhb[oi]])
                B.dma(kh_sb[oi][c * 64:(c + 1) * 64, :], KbT[8 * c + hp, e * 64:(e + 1) * 64, :],
                      reads=[scr["KbT"]], writes=[khb[oi]] if c == 0 else (), accum=() if c == 0 else [khb[oi]])

        diff_vload(0)
        diff_qkload(0)

        it2 = [0]

        def swa_loads(kv):
            li = kv % 2
            B.dma(ka_sb[li][:, :], KaT[kv, :, :], reads=[scr["KaT"]], writes=[kab[li]])
            B.dma(qa_sb[li][:, :, :], QaT[4 * kv:4 * kv + 4, :, :].rearrange("c p t -> p c t"), reads=[scr["QaT"]], writes=[qab[li]])

        def swa_front(kv, i):
            li = kv % 2
            q_ap = qa_sb[li][:, :, i * 128:(i + 1) * 128]
            par_ = it2[0] % 2
            it2[0] += 1
            sb_ = (0, 1) if par_ == 0 else (2, 3)
            ob_ = (4, 5) if par_ == 0 else (6, 7)
            ri = par_
            mprev = msk[:, 6, :] if i == 0 else msk[:, 4, :]
            mcur = msk[:, 5, :]

            def emit_s():
                T.matmul(bank[sb_[0]].rearrange("p (g q) -> p g q", g=4), lhsT=ka_sb[li][:, i * 128:(i + 1) * 128], rhs=q_ap, start=True, stop=False)
                T.matmul(bank[sb_[0]], lhsT=ident, rhs=mprev, start=False, stop=True)
                T.matmul(bank[sb_[1]].rearrange("p (g q) -> p g q", g=4), lhsT=ka_sb[li][:, (i + 1) * 128:(i + 2) * 128], rhs=q_ap, start=True, stop=False)
                return T.matmul(bank[sb_[1]], lhsT=ident, rhs=mcur, start=False, stop=True)
            mm_group(emit_s, [kab[li], qab[li], mskb, cbuf], sb_)
            pi_ = ptc[0] % 3
            ptc[0] += 1
            B.op(act, lambda: S.activation(out=pt[pi_][:, :], in_=psum[:, sb_[0] * 512:(sb_[0] + 2) * 512],
                                           func=AF.Exp, scale=SCALE_A),
                 reads=[bankb[sb_[0]], bankb[sb_[1]]], writes=[ptb[pi_]])

            def back():
                def emit_pv():
                    T.matmul(bank[ob_[0]], lhsT=va_sb[:, i, kv * 128:(kv + 1) * 128], rhs=pt[pi_][:, 0:512], start=True, stop=False)
                    T.matmul(bank[ob_[0]], lhsT=va_sb[:, i + 1, kv * 128:(kv + 1) * 128], rhs=pt[pi_][:, 512:1024], start=False, stop=True)
                    T.matmul(bank[ob_[1]], lhsT=ones1, rhs=pt[pi_][:, 0:512], start=True, stop=False)
                    return T.matmul(bank[ob_[1]], lhsT=ones1, rhs=pt[pi_][:, 512:1024], start=False, stop=True)
                mm_group(emit_pv, [vab, ptb[pi_], cbuf], ob_)
                B.op(dve, lambda: V.tensor_tensor(out=rd[ri][:, :].rearrange("p (g q) -> p g q", g=4),
                                                  in0=bank[ob_[1]].rearrange("p (g q) -> p g q", g=4),
                                                  in1=esk[:, 4 * kv:4 * kv + 4, :], op=ALU.add),
                     reads=[bankb[ob_[1]], eskb], writes=[rdb[ri]])
                B.op(act, lambda: S.activation(out=rd[ri][:, :], in_=rd[ri][:, :], func=AF.Ln), reads=[], writes=[rdb[ri]])
                B.op(act, lambda: S.activation(out=rd[ri][:, :], in_=rd[ri][:, :], func=AF.Exp, scale=-1.0), reads=[], writes=[rdb[ri]])
                B.op(dve, lambda: V.tensor_tensor(out=oa_sb[li][:, :, i * 128:(i + 1) * 128],
                                                  in0=bank[ob_[0]].rearrange("p (g q) -> p g q", g=4),
                                                  in1=rd[ri][:, :].rearrange("p (g q) -> p g q", g=4), op=ALU.mult),
                     reads=[bankb[ob_[0]], rdb[ri]], accum=[oab[li]] if i else (), writes=() if i else [oab[li]])
                if i == 7:
                    for t_ in range(2):
                        B.dma(catT[t_, :, 4 * kv:4 * kv + 4, :], oa_sb[li][:, :, t_ * 512:(t_ + 1) * 512], reads=[oab[li]],
                              accum=[scr["catT"]])
            return back

        swa_loads(0)
        swa_back = None
        for kv in range(4):
            if kv + 1 < 4:
                swa_loads(kv + 1)
            for i in range(8):
                nb_ = swa_front(kv, i)
                if swa_back is not None:
                    swa_back()
                swa_back = nb_
        swa_back()

        def diff_epilogue_a(h, t):
            B.op(dve, lambda: V.tensor_copy(out=oc[:, :], in_=psum[:, 4 * 512:6 * 512]), reads=[bankb[4], bankb[5]], writes=[ocb])
            B.op(act, lambda: S.activation(out=r1[:, :], in_=psum[:, 6 * 512:8 * 512], func=AF.Ln),
                 reads=[bankb[6], bankb[7]], writes=[r1b])
            B.op(act, lambda: S.activation(out=r1[:, :], in_=r1[:, :], func=AF.Exp, scale=-1.0), reads=[], writes=[r1b])
            B.op(dve, lambda: V.tensor_tensor(out=u1[:, :], in0=oc[:, 0:512], in1=r1[:, 0:512], op=ALU.mult),
                 reads=[ocb, r1b], writes=[u1b])
            B.op(dve, lambda: V.tensor_tensor(out=u2[:, :], in0=oc[:, 512:1024], in1=r1[:, 512:1024], op=ALU.mult),
                 reads=[ocb, r1b], writes=[u2b])
            B.op(dve, lambda: V.scalar_tensor_tensor(out=u1[:, :], in0=u2[:, :], scalar=neglam[:, 0:1], in1=u1[:, :],
                                                     op0=ALU.mult, op1=ALU.add), reads=[u2b, nlb], writes=[u1b])

        def diff_epilogue_b(h, t, sbank):
            oi = h % 2
            B.op(act, lambda: S.activation(out=sqd[:, :], in_=u1[:, :], func=AF.Square), reads=[u1b], writes=[sqdb])
            mm_group(lambda: T.matmul(bank[sbank], lhsT=onesn, rhs=sqd[:, :], start=True, stop=True), [sqdb, cbuf], [sbank])
            B.op(act, lambda: S.activation(out=rs[:, :], in_=bank[sbank], func=AF.Ln, bias=EPS, scale=1.0),
                 reads=[bankb[sbank]], writes=[rsb])
            B.op(act, lambda: S.activation(out=rs[:, :], in_=rs[:, :], func=AF.Exp, scale=-0.5), reads=[], writes=[rsb])
            B.op(dve, lambda: V.scalar_tensor_tensor(out=u1[:, :], in0=u1[:, :], scalar=1.0 - LAMBDA_INIT, in1=rs[:, :],
                                                     op0=ALU.mult, op1=ALU.mult), reads=[rsb], writes=[u1b])
            B.op(act, lambda: S.activation(out=od_sb[oi][:, t * 512:(t + 1) * 512], in_=u1[:, :], func=AF.Copy,
                                           scale=par[:, P_GSUB:P_GSUB + 1]),
                 reads=[u1b, cbuf], accum=[odb[oi]] if t else (), writes=() if t else [odb[oi]])
            if t == 1:
                for t_ in range(2):
                    B.dma(catT[t_, :, 16 + h, :], od_sb[oi][:, t_ * 512:(t_ + 1) * 512], reads=[odb[oi]], accum=[scr["catT"]])

        pending_epi = None
        for h in range(16):
            hp, e, oi, li = h // 2, h % 2, h % 2, (h // 2) % 2
            if e == 0 and hp + 1 < 8:
                diff_vload(hp + 1)
            if h + 1 < 16:
                diff_qkload(h + 1)
            for t in range(2):
                nkb = 8 + 4 * (t + 1)
                pend = None
                for kb in range(nkb + 1):
                    if kb < nkb:
                        sbk = (0, 1) if kb % 2 == 0 else (2, 3)
                        dj = kb - 8 - 4 * t

                        def emit_s():
                            for c in range(2):
                                ins = T.matmul(bank[sbk[c]], lhsT=kh_sb[oi][c * 64:(c + 1) * 64, kb * 128:(kb + 1) * 128],
                                               rhs=qh_sb[oi][c * 64:(c + 1) * 64, t * 512:(t + 1) * 512],
                                               start=True, stop=(dj < 0))
                            if dj >= 0:
                                for c in range(2):
                                    ins = T.matmul(bank[sbk[c]], lhsT=ident, rhs=msk[:, dj, :], start=False, stop=True)
                            return ins
                        mm_group(emit_s, [khb[oi], qhb[oi], mskb, cbuf], sbk)
                        pi_ = ptc[0] % 3
                        ptc[0] += 1
                        B.op(act, lambda: S.activation(
                            out=pt[pi_][:, :], in_=psum[:, sbk[0] * 512:(sbk[0] + 2) * 512], func=AF.Exp, scale=SCALE_B),
                            reads=[bankb[sbk[0]], bankb[sbk[1]]], writes=[ptb[pi_]])
                    if kb == 3 and pending_epi is not None:
                        pending_epi(0)
                        pending_epi = None
                    if pend is not None:
                        pkb, ppi = pend

                        def emit_pv():
                            onesm = onesv if pkb < 8 else ones1
                            for c in range(2):
                                T.matmul(bank[4 + c], lhsT=vb_sb[li][:, pkb, e * 128:(e + 1) * 128],
                                         rhs=pt[ppi][:, c * 512:(c + 1) * 512], start=(pkb == 0), stop=(pkb == nkb - 1))
                                ins = T.matmul(bank[6 + c], lhsT=onesm, rhs=pt[ppi][:, c * 512:(c + 1) * 512],
                                               start=(pkb == 0), stop=(pkb == nkb - 1))
                            return ins
                        ob4 = [bankb[4], bankb[5], bankb[6], bankb[7]]
                        if pkb == 0:
                            B.op(pe, emit_pv, reads=[vbb[li], ptb[ppi], cbuf], writes=ob4)
                        else:
                            B.op(pe, emit_pv, reads=[vbb[li], ptb[ppi], cbuf], accum=ob4)
                    pend = (kb, pi_) if kb < nkb else None
                diff_epilogue_a(h, t)
                B.pool.wait_ev(pe.sem, pe.sem.n)
                conv_step(1)
                pending_epi = (lambda h=h, t=t: (lambda sbank: diff_epilogue_b(h, t, sbank)))()
        pending_epi(0)
        B.barrier()
        p2.close()

        conv_finish()
        p3 = contextlib.ExitStack()
        hres = B.sb(p3, [128, 4, D], F32, "hres")
        hb = [Buf() for _ in range(4)]
        XT = B.sb(p3, [128, 32, 512], BF16, "XT")
        XTb = Buf()
        actT = [B.sb(p3, [128, 8, 512], BF16, "actT") for _ in range(2)]
        actTb = [Buf() for _ in range(2)]
        xbf3 = B.sb(p3, [128, 4096], BF16, "xbf3")
        xbf3b = Buf()
        relu = B.sb(p3, [128, 2048], F32, "relu")
        junk3 = relu[:, 0:1024].bitcast(BF16)
        relub = Buf()
        pin = B.sb(p3, [128, 4, PLE], F32, "pin")
        pinb = Buf()
        pbf = B.sb(p3, [128, 4, PLE], BF16, "pbf")
        pbfb = Buf()
        pT = B.sb(p3, [128, 2, 512], BF16, "pT")
        pTb = Buf()
        gpl = [B.sb(p3, [128, 512], F32, "gpl") for _ in range(2)]
        gplb = [Buf() for _ in range(2)]
        sg = [B.sb(p3, [128, 512], F32, "sg") for _ in range(2)]
        sgb = [Buf() for _ in range(2)]
        peb_ = [B.sb(p3, [128, 512], F32, "peb") for _ in range(4)]
        pebb = [Buf() for _ in range(4)]
        g3 = [0]

        def nextbanks4():
            bs = (0, 1, 2, 3) if g3[0] % 2 == 0 else (4, 5, 6, 7)
            g3[0] += 1
            return bs

        def tile_loads_early(t):
            B.dma(pin[:, :, :], p_own[t * 512:(t + 1) * 512, :].rearrange("(c p) d -> p c d", p=128), writes=[pinb])
            B.dma(XT[:, :, :], catT[t, :, :, :], reads=[scr["catT"]], writes=[XTb])

        def tile_loads_h(t, tcs=range(4)):
            for tc in tcs:
                r0 = t * 512 + tc * 128
                B.dma(hres[:, tc, :], x_own[r0:r0 + 128, :], writes=[hb[tc]])

        tile_loads_early(0)
        tile_loads_h(0)
        for t in range(2):

            def tokmajor_block(c0):
                bs = nextbanks4()
                bb = [bankb[i] for i in bs]
                for kh in range(2):
                    j, w, wb = ws.get()

                    def emit():
                        for k in range(16):
                            for tc in range(4):
                                ins = T.matmul(bank[bs[tc]], lhsT=XT[:, kh * 16 + k, tc * 128:(tc + 1) * 128],
                                               rhs=w[:, k, :], start=(kh == 0 and k == 0), stop=(kh == 1 and k == 15))
                        return ins
                    if kh == 0:
                        B.op(pe, emit, reads=[XTb, wb], writes=bb)
                    else:
                        B.op(pe, emit, reads=[XTb, wb], accum=bb)
                    ws.release(j)
                for tc in range(4):
                    B.op(dve, lambda: V.tensor_tensor(out=hres[:, tc, c0:c0 + 512], in0=hres[:, tc, c0:c0 + 512],
                                                      in1=bank[bs[tc]], op=ALU.add),
                         reads=[bankb[bs[tc]]], accum=[hb[tc]])

            B.op(act, lambda: S.copy(out=pbf[:, :, :], in_=pin[:, :, :]), reads=[pinb], writes=[pbfb])
            psb = bank[0].bitcast(BF16)

            def emit_pt():
                for tc in range(4):
                    for kc in range(2):
                        ins = T.transpose(out=psb[:, (kc * 4 + tc) * 128:(kc * 4 + tc + 1) * 128],
                                          in_=pbf[:, tc, kc * 128:(kc + 1) * 128], identity=ident)
                return ins
            mm_group(emit_pt, [pbfb, cbuf], [0])
            B.op(act, lambda: S.copy(out=pT[:, :, :], in_=psb.rearrange("p (k t) -> p k t", k=2)), reads=[bankb[0]], writes=[pTb])

            pss = newstat(32)
            pssb = Buf()
            for cb in range(8):
                j0, w0, wb0 = ws.get()
                bs = nextbanks4()

                def emit():
                    for tc in range(4):
                        for kc in range(2):
                            ins = T.matmul(bank[bs[tc]], lhsT=pT[:, kc, tc * 128:(tc + 1) * 128], rhs=w0[:, kc, :],
                                           start=(kc == 0), stop=(kc == 1))
                    return ins
                mm_group(emit, [pTb, wb0], bs)
                ws.release(j0)
                for tc in range(4):
                    B.op(act, lambda tc=tc: S.activation(out=junk3[:, 0:512], in_=bank[bs[tc]], func=AF.Square, scale=1.0 / 64.0,
                                                         accum_out=pss[:, tc * 8 + cb: tc * 8 + cb + 1]),
                         reads=[bankb[bs[tc]]], accum=[pssb])
            prs = newstat(12)
            prsb = Buf()
            B.op(dve, lambda: V.reduce_sum(out=prs[:, 0:4], in_=pss.rearrange("p (t c) -> p t c", t=4), axis=AX.X),
                 reads=[pssb], writes=[prsb])
            B.op(act, lambda: S.activation(out=prs[:, 4:8], in_=prs[:, 0:4], func=AF.Sqrt, bias=EPS, scale=1.0), reads=[prsb], accum=[prsb])
            B.op(dve, lambda: V.reciprocal(out=prs[:, 8:12], in_=prs[:, 4:8]), reads=[prsb], accum=[prsb])

            for cb in range(8):
                tokmajor_block(cb * 512)

            def h_half(tc, hf):
                return hres[:, tc, hf * 2048:(hf + 1) * 2048], hb[tc]

            build_xT((xbf3, xbf3b, junk3), h_half, 4, P_GMLP, XT, XTb, 0, True, [(0, 1, 2, 3), (4, 5, 6, 7)])

            def up_stage(f):
                ai = f % 2
                for cbl in range(2):
                    bs = nextbanks4()
                    bb = [bankb[i] for i in bs]
                    for kh in range(2):
                        j, w, wb = ws.get()

                        def emit():
                            for k in range(16):
                                for cc in range(4):
                                    ins = T.matmul(bank[bs[cc]], lhsT=w[:, k, cc * 128:(cc + 1) * 128], rhs=XT[:, kh * 16 + k, :],
                                                   start=(kh == 0 and k == 0), stop=(kh == 1 and k == 15))
                            return ins
                        if kh == 0:
                            B.op(pe, emit, reads=[XTb, wb], writes=bb)
                        else:
                            B.op(pe, emit, reads=[XTb, wb], accum=bb)
                        ws.release(j)
                    B.op(act, lambda: S.activation(out=relu[:, :], in_=psum[:, bs[0] * 512:(bs[0] + 4) * 512], func=AF.Relu),
                         reads=bb, writes=[relub])
                    B.op(dve, lambda: V.tensor_tensor(out=actT[ai][:, cbl * 4:(cbl + 1) * 4, :],
                                                      in0=relu[:, :].rearrange("p (c t) -> p c t", c=4),
                                                      in1=relu[:, :].rearrange("p (c t) -> p c t", c=4), op=ALU.mult),
                         reads=[relub], accum=[actTb[ai]] if cbl else (), writes=() if cbl else [actTb[ai]])

            def down_stage(f):
                ai = f % 2
                for cq in range(4):
                    j0, w0, wb0 = ws.get()
                    for half in range(2):
                        c0 = cq * 1024 + half * 512
                        bs = nextbanks4()

                        def emit():
                            for k in range(8):
                                for tc in range(4):
                                    ins = T.matmul(bank[bs[tc]], lhsT=actT[ai][:, k, tc * 128:(tc + 1) * 128],
                                                   rhs=w0[:, k, half * 512:(half + 1) * 512], start=(k == 0), stop=(k == 7))
                            return ins
                        mm_group(emit, [actTb[ai], wb0], bs)
                        for tc in range(4):
                            B.op(dve, lambda tc=tc: V.tensor_tensor(out=hres[:, tc, c0:c0 + 512], in0=hres[:, tc, c0:c0 + 512],
                                                                    in1=bank[bs[tc]], op=ALU.add),
                                 reads=[bankb[bs[tc]]], accum=[hb[tc]])
                    ws.release(j0)

            up_stage(0)
            for f in range(16):
                if f + 1 < 16:
                    up_stage(f + 1)
                down_stage(f)

            build_xT((xbf3, xbf3b, junk3), h_half, 4, None, XT, XTb, 0, False, [(0, 1, 2, 3), (4, 5, 6, 7)])
            for cb in range(8):
                c0 = cb * 512
                gli = cb % 2
                B.dma(gpl[gli][:, :], gple[:, c0:c0 + 512], writes=[gplb[gli]])
                jp, wp, wpb = ws.get()
                for tp in range(2):
                    pbk = (4, 5) if tp == 0 else (6, 7)

                    def emit_p():
                        for ii in range(2):
                            tc = tp * 2 + ii
                            for kc in range(2):
                                ins = T.matmul(bank[pbk[ii]], lhsT=pT[:, kc, tc * 128:(tc + 1) * 128], rhs=wp[:, kc, :],
                                               start=(kc == 0), stop=(kc == 1))
                        return ins
                    mm_group(emit_p, [pTb, wpb], pbk)
                    for ii in range(2):
                        tc = tp * 2 + ii
                        B.op(dve, lambda: V.scalar_tensor_tensor(
                            out=peb_[tc][:, :], in0=bank[pbk[ii]], scalar=prs[:, 8 + tc: 9 + tc], in1=gpl[gli][:, :],
                            op0=ALU.mult, op1=ALU.mult), reads=[bankb[pbk[ii]], prsb, gplb[gli]], writes=[pebb[tc]])
                ws.release(jp)
                bs = (0, 1, 2, 3)
                bb = [bankb[i] for i in bs]
                for kh in range(2):
                    j, w, wb = ws.get()

                    def emit_g():
                        for k in range(16):
                            for tc in range(4):
                                ins = T.matmul(bank[bs[tc]], lhsT=XT[:, kh * 16 + k, tc * 128:(tc + 1) * 128], rhs=w[:, k, :],
                                               start=(kh == 0 and k == 0), stop=(kh == 1 and k == 15))
                        return ins
                    if kh == 0:
                        B.op(pe, emit_g, reads=[XTb, wb], writes=bb)
                    else:
                        B.op(pe, emit_g, reads=[XTb, wb], accum=bb)
                    ws.release(j)
                for tc in range(4):
                    ii = tc % 2
                    B.op(act, lambda: S.activation(out=sg[ii][:, :], in_=bank[bs[tc]], func=AF.Sigmoid),
                         reads=[bankb[bs[tc]]], writes=[sgb[ii]])
                    B.op(dve, lambda: V.tensor_tensor(out=sg[ii][:, :], in0=sg[ii][:, :], in1=peb_[tc][:, :], op=ALU.mult),
                         reads=[pebb[tc]], writes=[sgb[ii]])
                    B.op(dve, lambda: V.tensor_tensor(out=hres[:, tc, c0:c0 + 512], in0=hres[:, tc, c0:c0 + 512],
                                                      in1=sg[ii][:, :], op=ALU.add),
                         reads=[sgb[ii]], accum=[hb[tc]])
            if t + 1 < 2:
                tile_loads_early(t + 1)
            outb = Buf()
            for tc in range(4):
                r0 = t * 512 + tc * 128
                B.dma(out[r0:r0 + 128, :], hres[:, tc, :], reads=[hb[tc]], accum=[outb])
                if t + 1 < 2:
                    tile_loads_h(t + 1, [tc])
        B.barrier()
        p3.close()
    return nc


def _rope_tables(pos, dim):
    pos = pos.astype(np.float64)
    inv = 1.0 / (10000.0 ** (np.arange(0, dim, 2, dtype=np.float64) / float(dim)))
    ang = pos[:, None] * inv[None, :]
    return np.stack([np.cos(ang), np.sin(ang)], axis=0).astype(np.float32)


def _rope_pack(ra, rb):
    def lay(t):
        n = t.shape[1] // 128
        return t.reshape(2, n, 128, t.shape[2]).transpose(2, 0, 1, 3).reshape(128, -1)
    return np.ascontiguousarray(np.concatenate(
        [lay(ra[:, 0:128]), lay(rb[:, 0:1024]), lay(ra[:, 128:1152]), lay(rb[:, 1024:2048])], axis=1).astype(np.float32))


def _masks(half):
    kl = np.arange(128)[:, None]
    m = np.zeros((128, 7, 512), np.float32)
    q = np.arange(512)[None, :]
    for j in range(4):
        kg = j * 128 + kl
        m[:, j, :] = np.where(kg <= q, 0.0, NEG)
    ql = np.arange(128)[None, :]
    prev = np.where(kl > ql, 0.0, NEG)
    cur = np.where(kl <= ql, 0.0, NEG)
    m[:, 4, :] = np.tile(prev, (1, 4))
    m[:, 5, :] = np.tile(cur, (1, 4))
    m[:, 6, :] = np.tile(prev, (1, 4)) if half == 1 else NEG
    return m.astype(BF)


_CACHE = {}


def kernel(x, p, attn_norm_g, w_in, swa_q_norm_g, swa_k_norm_g, swa_sinks, diff_q_norm_g, diff_k_norm_g,
           diff_lambda_q1, diff_lambda_k1, diff_lambda_q2, diff_lambda_k2, diff_subln_g, w_o, mlp_norm_g,
           w_up, w_down, w_ple_proj, ple_norm_g, w_ple_gate, _debug=False):
    f = lambda a: np.ascontiguousarray(np.asarray(a, dtype=np.float32))
    x = f(x); p = f(p)
    w_in_, w_o_, w_up_, w_down_, w_ple_, w_gate_ = f(w_in)[0], f(w_o)[0], f(w_up)[0], f(w_down)[0], f(w_ple_proj)[0], f(w_ple_gate)[0]
    rep = lambda v, n=128: np.tile(f(v).reshape(1, -1), (n, 1))
    gple = np.ascontiguousarray(rep(ple_norm_g[0]))
    in_maps = []
    for c in range(8):
        b, half = c // 2, c % 2
        par = np.zeros((128, NPAR), np.float32)
        par[:, P_GATT:P_GATT + 32] = f(attn_norm_g)[0].reshape(32, 128).T
        par[:, P_GMLP:P_GMLP + 32] = f(mlp_norm_g)[0].reshape(32, 128).T
        par[:, P_GQA:P_GQA + 128] = rep(swa_q_norm_g[0])
        par[:, P_GKA:P_GKA + 128] = rep(swa_k_norm_g[0])
        par[:, P_GQB:P_GQB + 64] = rep(diff_q_norm_g[0])
        par[:, P_GKB:P_GKB + 64] = rep(diff_k_norm_g[0])
        par[:, P_GSUB] = f(diff_subln_g)[0]
        par[:, P_SINK:P_SINK + 16] = rep(swa_sinks[0])
        par[:, P_LQ1:P_LQ1 + 64] = rep(diff_lambda_q1[0])
        par[:, P_LK1:P_LK1 + 64] = rep(diff_lambda_k1[0])
        par[:, P_LQ2:P_LQ2 + 64] = rep(diff_lambda_q2[0])
        par[:, P_LK2:P_LK2 + 64] = rep(diff_lambda_k2[0])
        par[:, P_VALID] = float(half)
        pos0 = half * 1024
        posA = np.arange(pos0 - 128, pos0 + 1024)
        posB = np.concatenate([np.arange(0, 1024), np.arange(pos0, pos0 + 1024)])
        mats = np.zeros((128, 4, 128), np.float32)
        mats[:, 0, :] = np.eye(128)
        mats[:, 1, :] = 1.0
        mats[:, 2, :] = float(half)
        mats[:, 3, :] = 1.0 / 128.0
        in_maps.append({
            "x_own": np.ascontiguousarray(x[b, pos0:pos0 + 1024]),
            "x_ctx": np.ascontiguousarray(x[b, 0:1024]) if half == 1 else np.zeros((1024, D), np.float32),
            "p_own": np.ascontiguousarray(p[0, b, pos0:pos0 + 1024]),
            "w_in": w_in_, "w_o": w_o_, "w_up": w_up_, "w_down": w_down_, "w_ple": w_ple_, "w_gate": w_gate_,
            "params": par, "gple": gple,
            "rope": _rope_pack(_rope_tables(posA, 128), _rope_tables(posB, 64)),
            "mats": mats.astype(BF), "masks": _masks(half),
        })
    key = bool(_debug)
    if key not in _CACHE:
        _CACHE[key] = build_program(debug=_debug)
    nc = _CACHE[key]
    res = run_bass_kernel_spmd(nc, in_maps, core_ids=list(range(8)))
    outs = [r["out"] for r in res.results]
    full = np.zeros((4, 2048, D), np.float32)
    for c in range(8):
        b, half = c // 2, c % 2
        full[b, half * 1024:(half + 1) * 1024] = outs[c]
    if _debug:
        return full, res.results
    return full
```
